# Optimizing a Trainium2 kernel written in Bass

```python
import jax, jax.numpy as jnp
from jax import lax
import numpy as np

D_MODEL = 2048
BATCH = 2
SEQ = 8192
DEPTH = 1
DEC_BATCH = 128
DEC_SEQ = 4
PAST_LEN = 16384
PAGE_SIZE = 128

HEAD_DIM = 64
N_HEADS = D_MODEL // (2 * HEAD_DIM)
N_KV_HEADS = 4
GROUP = N_HEADS // N_KV_HEADS
ATT_W = N_HEADS * HEAD_DIM
KV_W = N_KV_HEADS * HEAD_DIM
WINDOW = 128
BLOCK = WINDOW
D_CONV = D_MODEL // 2
CONV_W = 31
N_BRANCH = 2
D_FF = 4 * D_MODEL
IN_W = ATT_W + 2 * KV_W + 2 * D_CONV + N_BRANCH * D_MODEL
EPS = 1e-6
NEG = -1e30
ATTN_SCALE = HEAD_DIM ** -0.5

kernel_name = "hybrid_swa_sink_conformer_gated_decode_step"


def _rmsnorm(x, g):
    xf = x.astype(jnp.float32)
    y = xf * lax.rsqrt(jnp.mean(xf * xf, axis=-1, keepdims=True) + EPS)
    return (y * g.astype(jnp.float32)).astype(x.dtype)


def _layernorm(x, g, b):
    xf = x.astype(jnp.float32)
    mu = jnp.mean(xf, axis=-1, keepdims=True)
    xc = xf - mu
    y = xc * lax.rsqrt(jnp.mean(xc * xc, axis=-1, keepdims=True) + EPS)
    return (y * g.astype(jnp.float32) + b.astype(jnp.float32)).astype(x.dtype)


def _project(h, w_in, b_gate):
    lead = h.shape[:-1]
    p = h @ w_in
    q, k, v, u, gl = jnp.split(
        p, [ATT_W, ATT_W + KV_W, ATT_W + 2 * KV_W, ATT_W + 2 * KV_W + 2 * D_CONV], axis=-1)
    q = q.reshape(*lead, N_KV_HEADS, GROUP, HEAD_DIM)
    k = k.reshape(*lead, N_KV_HEADS, HEAD_DIM)
    v = v.reshape(*lead, N_KV_HEADS, HEAD_DIM)
    ua, ub = jnp.split(u, 2, axis=-1)
    u = ua * jax.nn.sigmoid(ub)
    gates = jax.nn.sigmoid(gl + b_gate).reshape(*lead, N_BRANCH, D_MODEL)
    return q, k, v, u, gates


def _sink_attention(q, k, v, mask, sink):
    s = jnp.einsum('nqkgd,nskd->nkgqs', q, k,
                   preferred_element_type=jnp.float32) * ATTN_SCALE
    s = jnp.where(mask, s, NEG)
    sk = sink.astype(jnp.float32).reshape(1, N_KV_HEADS, GROUP, 1, 1)
    m = jnp.maximum(jnp.max(s, axis=-1, keepdims=True), sk)
    e = jnp.exp(s - m)
    den = jnp.sum(e, axis=-1, keepdims=True) + jnp.exp(sk - m)
    p = (e / den).astype(v.dtype)
    o = jnp.einsum('nkgqs,nskd->nqkgd', p, v)
    return o.reshape(o.shape[0], o.shape[1], ATT_W)


def _prompt_attention(q, k, v, sink):
    B, S = q.shape[0], q.shape[1]
    nb = S // BLOCK
    qb = q.reshape(B * nb, BLOCK, N_KV_HEADS, GROUP, HEAD_DIM)

    def band(t):
        tb = t.reshape(B, nb, BLOCK, N_KV_HEADS, HEAD_DIM)
        prev = jnp.concatenate([jnp.zeros_like(tb[:, :1]), tb[:, :-1]], axis=1)
        return jnp.concatenate([prev, tb], axis=2).reshape(B * nb, 2 * BLOCK, N_KV_HEADS, HEAD_DIM)

    i = jnp.arange(BLOCK)[:, None]
    j = jnp.arange(2 * BLOCK)[None, :]
    diff = i + BLOCK - j
    blk = jnp.arange(nb)[:, None, None]
    mask = (diff >= 0) & (diff < WINDOW) & (blk * BLOCK - BLOCK + j >= 0)
    mask = jnp.broadcast_to(mask[None], (B, nb, BLOCK, 2 * BLOCK)).reshape(
        B * nb, 1, 1, BLOCK, 2 * BLOCK)
    o = _sink_attention(qb, band(k), band(v), mask, sink)
    return o.reshape(B, S, ATT_W)


def _sample_attention(q, k_all, v_all, sink):
    T = q.shape[1]
    qpos = PAST_LEN + jnp.arange(T)
    kpos = jnp.concatenate([PAST_LEN - WINDOW + jnp.arange(WINDOW), qpos])
    diff = qpos[:, None] - kpos[None, :]
    mask = ((diff >= 0) & (diff < WINDOW))[None, None, None]
    return _sink_attention(q, k_all, v_all, mask, sink)


def _conv_tail(ubuf, conv_w, conv_b, cln_g, cln_b, w_conv_o):
    c = lax.conv_general_dilated(ubuf, conv_w[:, None, :], (1,), 'VALID',
                                 dimension_numbers=('NWC', 'WIO', 'NWC'),
                                 feature_group_count=D_CONV) + conv_b
    c = jax.nn.silu(_layernorm(c, cln_g, cln_b))
    return c @ w_conv_o


def _finish(x, y_att, y_conv, gates, w_attn_o, w_out, norm2_g, w_up, w_down):
    ya = y_att @ w_attn_o
    mix = (gates[..., 0, :] * ya + gates[..., 1, :] * y_conv) @ w_out
    h = x + mix
    z = _rmsnorm(h, norm2_g) @ w_up
    return h + jnp.square(jax.nn.relu(z)) @ w_down


def setup_inputs(seed: int = 0) -> dict:
    key = jax.random.key(seed)
    ks = jax.random.split(key, 24)
    f32 = jnp.float32

    def nrm(k, shape, scale):
        return jax.random.normal(k, shape, f32) * scale

    return {
        "x_prompt": nrm(ks[0], (BATCH, SEQ, D_MODEL), 1.0),
        "x_sample": nrm(ks[1], (DEC_BATCH, DEC_SEQ, D_MODEL), 1.0),
        "cache_k": nrm(ks[2], (DEPTH, DEC_BATCH, WINDOW, N_KV_HEADS, HEAD_DIM), 1.0),
        "cache_v": nrm(ks[3], (DEPTH, DEC_BATCH, WINDOW, N_KV_HEADS, HEAD_DIM), 1.0),
        "state_conv": nrm(ks[4], (DEPTH, DEC_BATCH, CONV_W - 1, D_CONV), 0.5),
        "norm1_g": 1.0 + nrm(ks[5], (DEPTH, D_MODEL), 0.02),
        "w_in": nrm(ks[6], (DEPTH, D_MODEL, IN_W), D_MODEL ** -0.5),
        "b_gate": nrm(ks[7], (DEPTH, N_BRANCH * D_MODEL), 0.01),
        "sink": nrm(ks[8], (DEPTH, N_HEADS), 0.5),
        "w_attn_o": nrm(ks[9], (DEPTH, ATT_W, D_MODEL), ATT_W ** -0.5),
        "conv_w": nrm(ks[10], (DEPTH, CONV_W, D_CONV), CONV_W ** -0.5),
        "conv_b": nrm(ks[11], (DEPTH, D_CONV), 0.01),
        "cln_g": 1.0 + nrm(ks[12], (DEPTH, D_CONV), 0.02),
        "cln_b": nrm(ks[13], (DEPTH, D_CONV), 0.01),
        "w_conv_o": nrm(ks[14], (DEPTH, D_CONV, D_MODEL), D_CONV ** -0.5),
        "w_out": nrm(ks[15], (DEPTH, D_MODEL, D_MODEL), D_MODEL ** -0.5),
        "norm2_g": 1.0 + nrm(ks[16], (DEPTH, D_MODEL), 0.02),
        "w_up": nrm(ks[17], (DEPTH, D_MODEL, D_FF), D_MODEL ** -0.5),
        "w_down": nrm(ks[18], (DEPTH, D_FF, D_MODEL), D_FF ** -0.5),
        "norm_f_g": 1.0 + nrm(ks[19], (D_MODEL,), 0.02),
    }


def reference(x_prompt, x_sample, cache_k, cache_v, state_conv, norm1_g, w_in, b_gate, sink,
              w_attn_o, conv_w, conv_b, cln_g, cln_b, w_conv_o, w_out, norm2_g, w_up, w_down,
              norm_f_g):
    hp, hs = x_prompt, x_sample
    kp_l, vp_l, cp_l, ks_l, vs_l, cs_l = [], [], [], [], [], []
    for l in range(DEPTH):
        n = _rmsnorm(hp, norm1_g[l])
        q, k, v, u, g = _project(n, w_in[l], b_gate[l])
        ya = _prompt_attention(q, k, v, sink[l])
        ubuf = jnp.pad(u, ((0, 0), (CONV_W - 1, 0), (0, 0)))
        yc = _conv_tail(ubuf, conv_w[l], conv_b[l], cln_g[l], cln_b[l], w_conv_o[l])
        hp = _finish(hp, ya, yc, g, w_attn_o[l], w_out[l], norm2_g[l], w_up[l], w_down[l])
        kp_l.append(k[:, -WINDOW:])
        vp_l.append(v[:, -WINDOW:])
        cp_l.append(u[:, -(CONV_W - 1):])

        n = _rmsnorm(hs, norm1_g[l])
        q, k, v, u, g = _project(n, w_in[l], b_gate[l])
        k_all = jnp.concatenate([cache_k[l].astype(k.dtype), k], axis=1)
        v_all = jnp.concatenate([cache_v[l].astype(v.dtype), v], axis=1)
        ya = _sample_attention(q, k_all, v_all, sink[l])
        ubuf = jnp.concatenate([state_conv[l].astype(u.dtype), u], axis=1)
        yc = _conv_tail(ubuf, conv_w[l], conv_b[l], cln_g[l], cln_b[l], w_conv_o[l])
        hs = _finish(hs, ya, yc, g, w_attn_o[l], w_out[l], norm2_g[l], w_up[l], w_down[l])
        ks_l.append(k_all[:, -WINDOW:])
        vs_l.append(v_all[:, -WINDOW:])
        cs_l.append(ubuf[:, -(CONV_W - 1):])

    y_prompt = _rmsnorm(hp, norm_f_g)
    y_sample = _rmsnorm(hs, norm_f_g)
    return (y_prompt, y_sample, jnp.stack(kp_l), jnp.stack(vp_l), jnp.stack(cp_l),
            jnp.stack(ks_l), jnp.stack(vs_l), jnp.stack(cs_l))
```

```python
import numpy as np
from contextlib import ExitStack
import concourse.bass as bass
import concourse.mybir as mybir
from concourse.bass_utils import run_bass_kernel_spmd

F32 = mybir.dt.float32
BF16 = mybir.dt.bfloat16
AF = mybir.ActivationFunctionType
ALU = mybir.AluOpType

D = 2048
NKC = 16
IN_W = 7680
DFF = 8192
EPS = 1e-6
NEG = -30000.0
SCALE = 0.125
NCORES = 8
NGROUPS = 4
MO = 128
NCOL = 704
NMAIN = 576
V_G1, V_G2, V_BG, V_CW, V_CB, V_LG, V_LB, V_SK, V_N = 0, 16, 32, 64, 312, 320, 328, 336, 352


class _Rec:
    def __getattr__(self, name):
        def f(*args, **kwargs):
            return (name, args, kwargs)
        return f


class Prog:
    ENG = ["pe", "act", "dve", "pool", "sp"]

    def __init__(self, nc, es):
        self.nc, self.es = nc, es
        self.streams = {e: [] for e in self.ENG}
        self.semh = {}
        self.cnt = {}
        self.seen = {e: {} for e in self.ENG}
        self.res = {}
        for e in self.ENG:
            self._sem("e_" + e)

    def _sem(self, name):
        if name not in self.semh:
            self.semh[name] = self.es.enter_context(self.nc.semaphore(name))
            self.cnt[name] = 0
        return self.semh[name]

    def _r(self, k):
        if k not in self.res:
            self.res[k] = [{}, {}]
        return self.res[k]

    @staticmethod
    def _merge(d, s):
        for k, v in s.items():
            if d.get(k, 0) < v:
                d[k] = v

    def _deps(self, eng, reads, writes):
        deps = {}
        for k in reads:
            self._merge(deps, self._r(k)[0])
        for k in writes:
            r = self._r(k)
            self._merge(deps, r[0])
            self._merge(deps, r[1])
        waits = []
        my = "e_" + eng
        for s, v in deps.items():
            if s == my and eng == "pe":
                continue
            if self.seen[eng].get(s, 0) >= v:
                continue
            self.seen[eng][s] = v
            waits.append((s, v))
        return waits

    def _commit(self, me, reads, writes):
        for k in reads:
            self._merge(self._r(k)[1], me)
        for k in writes:
            r = self._r(k)
            r[0] = dict(me)
            r[1] = {}

    def op(self, eng, fn, reads=(), writes=(), inc=True):
        waits = self._deps(eng, reads, writes)
        my = "e_" + eng
        val = self.cnt[my] + 1
        if inc:
            self.cnt[my] = val
        self._commit({my: val}, reads, writes)
        self.streams[eng].append((waits, fn(_Rec()), (my, 1) if inc else None))

    def dma(self, q, fn, reads=(), writes=(), sem="d_misc", merge=False):
        self._sem(sem)
        waits = self._deps(q, reads, () if merge else writes)
        prev = self.cnt[sem]
        if prev > 0 and self.seen[q].get(sem, 0) < prev:
            self.seen[q][sem] = prev
            waits.append((sem, prev))
        self.cnt[sem] += 16
        me = {sem: self.cnt[sem]}
        if merge:
            for k in reads:
                self._merge(self._r(k)[1], me)
            for k in writes:
                self._merge(self._r(k)[0], me)
        else:
            self._commit(me, reads, writes)
        self.streams[q].append((waits, fn(_Rec()), (sem, 16)))

    def alias(self, new_keys, old_keys):
        deps = {}
        for k in old_keys:
            r = self._r(k)
            self._merge(deps, r[0])
            self._merge(deps, r[1])
        for k in new_keys:
            r = self._r(k)
            self._merge(r[1], deps)

    def emit(self):
        nc = self.nc
        fin = [(s, v) for s, v in self.cnt.items() if v > 0]
        self.streams["sp"].append((fin, None, None))
        with nc.Block() as block:
            def run(engobj, name):
                for waits, fn, inc in self.streams[name]:
                    for s, v in waits:
                        engobj.wait_ge(self.semh[s], v)
                    if fn is None:
                        continue
                    ins = getattr(engobj, fn[0])(*fn[1], **fn[2])
                    if inc is not None:
                        ins.then_inc(self.semh[inc[0]], inc[1])

            @block.tensor
            def _(e):
                run(e, "pe")

            @block.scalar
            def _(e):
                run(e, "act")

            @block.vector
            def _(e):
                run(e, "dve")

            @block.gpsimd
            def _(e):
                run(e, "pool")

            @block.sync
            def _(e):
                run(e, "sp")


def build(debug=False, ngroups=NGROUPS, glist=None):
    nc = bass.Bass("TRN2", target_bir_lowering=False)
    es = ExitStack()
    with es:
        def din(name, shape, dt=F32):
            return nc.dram_tensor(name, list(shape), dt, kind="ExternalInput").ap()

        def dout(name, shape, dt=F32):
            return nc.dram_tensor(name, list(shape), dt, kind="ExternalOutput").ap()

        xp = din("xp", [17 * 128, D])
        xs = din("xs", [64, D])
        ck = din("ck", [16, 128, 256])
        cv = din("cv", [16, 128, 256])
        st = din("st", [16, 30, 1024])
        masks = din("masks", [128, 5, 512])
        ident = din("ident", [128, 128])
        vecs = din("vecs", [128, V_N])
        gfb = din("gfb", [128, D])
        w_in = din("w_in", [D, IN_W])
        w_ao = din("w_ao", [1024, D])
        w_co = din("w_co", [1024, D])
        w_out = din("w_out", [D, D])
        w_up = din("w_up", [D, DFF])
        w_dn = din("w_dn", [DFF, D])

        yp = dout("yp", [2048, D])
        ys = dout("ys", [64, D])
        kp = dout("kp", [128, 256])
        vp = dout("vp", [128, 256])
        cp = dout("cp", [30, 1024])
        ksn = dout("ksn", [16, 128, 256])
        vsn = dout("vsn", [16, 128, 256])
        csn = dout("csn", [16, 30, 1024])

        def sb(name, shape, dt):
            return es.enter_context(nc.sbuf_tensor(name, list(shape), dt))

        nT = sb("nT", [128, 16, NCOL], BF16)
        R2 = sb("R2", [128, 5 * D], F32)
        R3 = sb("R3", [128, 16, NMAIN], BF16)
        R4 = sb("R4", [128, 16, NMAIN], BF16)
        uaT = sb("uaT", [128, 4, NCOL], F32)
        wsl = [sb(f"wsl{i}", [128, 16, 512], BF16) for i in range(3)]
        T1 = sb("T1", [128, 2 * D], F32)
        gfs = sb("gfs", [128, D], F32)
        vec = sb("vec", [128, V_N], F32)
        msk = sb("msk", [128, 5, 512], BF16)
        idb = sb("idb", [128, 128], BF16)
        onesb = sb("onesb", [128, 128], BF16)
        onesf = sb("onesf", [128, 128], F32)
        hvec = sb("hvec", [128, 16], F32)
        hbg = sb("hbg", [128, 32], F32)
        esk = sb("esk", [128, 16], F32)
        stat = sb("stat", [128, 16], F32)
        NPT = 5
        PT = [sb(f"PT{i}", [128, 512], BF16) for i in range(NPT)]
        esk2 = sb("esk2", [128, 2, 4], F32)
        rden = [sb(f"rden{i}", [128, 512], F32) for i in range(2)]
        gt = [sb(f"gt{i}", [128, 512], F32) for i in range(2)]
        rt = [sb(f"rt{i}", [128, 512], BF16) for i in range(2)]
        ckT = sb("ckT", [128, 2, 128], BF16)
        cvt = sb("cvt", [128, 256], BF16)
        cuT = sb("cuT", [128, 8, 32], BF16)
        qS = sb("qS", [128, 2, 256], BF16)

        ps = [es.enter_context(nc.psum_tensor(f"ps{i}", [128, 512], F32)) for i in range(8)]

        h = R2[:, :].rearrange("p (t c) -> p t c", c=D)
        R2b = R2[:, :].bitcast(BF16)
        o_ = 0
        qT = R2b[:, o_:o_ + 8 * NMAIN].rearrange("p (k c) -> p k c", c=NMAIN); o_ += 8 * NMAIN
        kT = R2b[:, o_:o_ + 2 * NCOL].rearrange("p (k c) -> p k c", c=NCOL); o_ += 2 * NCOL
        vtok = R2b[:, o_:o_ + 6 * 256].rearrange("p (t c) -> p t c", c=256); o_ += 6 * 256
        uT = R2b[:, o_:o_ + 8 * NCOL].rearrange("p (k c) -> p k c", c=NCOL); o_ += 8 * NCOL
        cT = R2b[:, o_:o_ + 8 * NMAIN].rearrange("p (k c) -> p k c", c=NMAIN); o_ += 8 * NMAIN
        assert o_ <= 20480, o_
        oT = R3[:, 0:8, :]
        sT = R3[:, 8:16, :]
        a2T = R3
        mixT = R4
        R4f = R4[:, :, :].rearrange("p k c -> p (k c)")
        KcT = R4f[:, 0:4096].rearrange("p (a b w) -> p a b w", a=2, b=16)
        vcb = R4f[:, 4096:8192].rearrange("p (b c) -> p b c", c=256)
        kcb = R4f[:, 8192:9216].rearrange("p (b c) -> p b c", c=256)
        stb = R4f[:, 0:4096].rearrange("p (b c) -> p b c", c=1024)
        ubT = R4f[:, 4096:4096 + 8 * 16 * 34].rearrange("p (k b r) -> p k b r", k=8, b=16)
        tmpA = uaT
        dg = uaT[:, :, :].rearrange("p k c -> p (k c)").bitcast(BF16)[:, 0:31 * 128].rearrange("p (j c) -> p j c", c=128)
        xin0 = T1[:, 0:D]
        T1b = T1[:, D:2 * D].bitcast(BF16)
        xn = [T1b[:, 0:D], T1b[:, D:2 * D]]
        acc = [T1[:, i * NMAIN:(i + 1) * NMAIN] for i in range(2)]
        sq = [T1[:, (2 + i) * NMAIN:(3 + i) * NMAIN] for i in range(2)]
        lnm = T1[:, 4 * NMAIN:5 * NMAIN]
        lnr = T1[:, 5 * NMAIN:6 * NMAIN]
        lnt = T1[:, 6 * NMAIN:7 * NMAIN]
        uat = T1[:, 0:2048].rearrange("p (o c) -> p o c", c=1024)
        stg = T1[:, 2048:3072]

        P = Prog(nc, es)
        roles = {}

        def switch(region, new_keys):
            old = roles.get(region, [])
            P.alias(new_keys, old)
            roles[region] = list(new_keys)

        bank_i = [0]

        reserved = set()

        def bank():
            while True:
                b = bank_i[0]
                bank_i[0] = (b + 1) % 8
                if b not in reserved:
                    return b

        slot_i = [0]

        def load_slab(src_fn):
            s = slot_i[0]
            slot_i[0] = (s + 1) % 3
            for i, (dst, src) in enumerate(src_fn(wsl[s])):
                P.dma("pool", lambda e, dst=dst, src=src: e.dma_start(out=dst, in_=src),
                      reads=[], writes=[f"w{s}"], sem=f"d_w{s}_{i}", merge=(i > 0))
            return s

        def slab_cols(w, c0, nk=16, r0=0):
            src = w[r0:r0 + nk * 128, c0:c0 + 512].rearrange("(k p) c -> p k c", p=128)
            return lambda t: [(t[:, 0:nk, :], src)]

        def slab_wo(cb):
            def f(t):
                prs = []
                for hh in range(2):
                    for a in range(2):
                        src = w_ao[a * 512 + hh * 256:a * 512 + hh * 256 + 256, cb * 512:(cb + 1) * 512]
                        prs.append((t[hh * 64:(hh + 1) * 64, a * 4:(a + 1) * 4, :],
                                    src.rearrange("(g d) n -> d g n", d=64)))
                return prs
            return f

        P.dma("sp", lambda e: e.dma_start(out=vec[:, :], in_=vecs[:, :]), writes=["vec"], sem="d_vec")
        P.dma("sp", lambda e: e.dma_start(out=gfs[:, :], in_=gfb[:, :]), writes=["gfs"], sem="d_gfs")
        P.dma("pool", lambda e: e.dma_start(out=msk[:, :, :], in_=masks[:, :, :]), writes=["msk"], sem="d_msk")
        P.dma("pool", lambda e: e.dma_start(out=idb[:, :], in_=ident[:, :]), writes=["idb"], sem="d_idb")
        P.op("dve", lambda e: e.memset(onesf[:, :], 1.0 / 1024.0), writes=["onesf"])
        P.op("dve", lambda e: e.memset(onesb[:, :], 1.0), writes=["onesb"])
        P.op("dve", lambda e: e.tensor_scalar(out=hvec[:, 0:16], in0=vec[:, V_LG:V_LG + 16], scalar1=0.5, scalar2=None,
                                              op0=ALU.mult), reads=["vec"], writes=["hvec"])
        P.op("dve", lambda e: e.tensor_scalar(out=hbg[:, :], in0=vec[:, V_BG:V_BG + 32], scalar1=0.5, scalar2=None,
                                              op0=ALU.mult), reads=["vec"], writes=["hbg"])
        P.op("act", lambda e: e.activation(out=esk[:, :], in_=vec[:, V_SK:V_SK + 16], func=AF.Exp),
             reads=["vec"], writes=["esk"])
        for a_ in range(2):
            for hh_ in range(2):
                hs_ = slice(hh_ * 64, (hh_ + 1) * 64)
                kv_ = 2 * a_ + hh_
                P.op("dve", lambda e, a_=a_, hs_=hs_, kv_=kv_: e.tensor_copy(out=esk2[hs_, a_, :], in_=esk[hs_, kv_ * 4:(kv_ + 1) * 4]),
                     reads=["esk"], writes=["esk2"])

        P.dma("sp", lambda e: e.dma_start(out=ksn[:, 0:124, :], in_=ck[:, 4:128, :]), sem="d_o1a")
        P.dma("sp", lambda e: e.dma_start(out=vsn[:, 0:124, :], in_=cv[:, 4:128, :]), sem="d_o1b")
        P.dma("sp", lambda e: e.dma_start(out=csn[:, 0:26, :], in_=st[:, 4:30, :]), sem="d_o1c")

        o2_i = [0]

        def o2sem():
            o2_i[0] += 1
            return f"d_o2_{o2_i[0] % 8}"

        stat_i = [0]

        def rms_stats(src_ap, rows, junk_ap, rkeys, jkeys):
            c = stat_i[0]
            stat_i[0] = (c + 1) % 8
            a = stat[:rows, 2 * c:2 * c + 1]
            b = stat[:rows, 2 * c + 1:2 * c + 2]
            k = f"stat{c}"
            P.op("act", lambda e: e.activation(out=junk_ap, in_=src_ap, func=AF.Square, accum_out=a),
                 reads=rkeys, writes=jkeys + [k])
            P.op("act", lambda e: e.activation(out=b, in_=a, func=AF.Sqrt, scale=1.0 / D, bias=EPS),
                 reads=[k], writes=[k + "b"])
            P.op("dve", lambda e: e.reciprocal(out=b, in_=b), reads=[k + "b"], writes=[k + "b"])
            return b, k + "b"

        SQD = float(np.sqrt(D))
        nTk = [f"nT{c}" for c in range(6)]

        def to_featmajor(src_bf, rows, col0, gcol, skey):
            for q4 in range(4):
                b = bank()
                for j in range(4):
                    kc = q4 * 4 + j
                    P.op("pe", lambda e, b=b, j=j, kc=kc: e.matmul(
                        ps[b][:, j * 128:j * 128 + rows], lhsT=src_bf[:rows, kc * 128:(kc + 1) * 128],
                        rhs=idb[:rows, :rows], start=True, stop=True),
                        reads=[skey, "idb"], writes=[f"ps{b}"], inc=(j == 3))
                for j in range(4):
                    kc = q4 * 4 + j
                    P.op("act", lambda e, b=b, j=j, kc=kc: e.activation(
                        out=nT[:, kc, col0:col0 + rows], in_=ps[b][:, j * 128:j * 128 + rows], func=AF.Identity,
                        scale=vec[:, gcol + kc:gcol + kc + 1]),
                        reads=[f"ps{b}", "vec"], writes=[f"nT{col0 // 128}"])

        def mm_group(out_fn, lhs_fn, rhs_fn, nk, rkeys):
            b = bank()
            for kc in range(nk):
                P.op("pe", lambda e, b=b, kc=kc: e.matmul(out_fn(b), lhsT=lhs_fn(kc), rhs=rhs_fn(kc),
                                                          start=(kc == 0), stop=(kc == nk - 1)),
                     reads=rkeys, writes=[f"ps{b}"], inc=(kc == nk - 1))
            return b

        T1A = ["xin0", "xn0", "xn1"]
        T1B = ["acc0", "acc1", "sq0", "sq1", "lnm", "lnr", "lnt"]
        R2A = ["qT", "kT", "vtok", "uT", "cT"] + [f"uTc{c}" for c in range(8)] + [f"cTc{c}" for c in range(8)]
        R2H = [f"h{t}" for t in range(5)]
        R3A = ["oT", "sT"]
        R3B = ["a2T"]
        R3C = ["uat0", "uat1", "stg", "stg2"]
        R4A = ["mixT"]
        R4B = ["KcT", "vcb", "kcb"]
        R4C = ["stb", "ubT"]

        def tiles_of(g):
            tl = []
            if g == 0:
                tl.append((xp[0:128, :], 128, 0, "halo"))
            for i in range(4):
                t = 4 * g + i
                tl.append((xp[128 * (t + 1):128 * (t + 2), :], 128, MO + 128 * i, "p"))
            if g == NGROUPS - 1:
                tl.append((xs[:, :], 64, MO + 512, "s"))
            return tl

        def p1_front(ti, tile):
            (src, rows, col0, kind) = tile
            xb = xn[ti % 2]
            xbk = f"xn{ti % 2}"
            P.dma("sp", lambda e: e.dma_start(out=xin0[:rows, :], in_=src), writes=["xin0"], sem="d_xin0")
            rs, rk = rms_stats(xin0[:rows, :], rows, xb[:rows, :], ["xin0"], [xbk])
            P.op("dve", lambda e: e.tensor_scalar(
                out=xb[:rows, :], in0=xin0[:rows, :], scalar1=rs, scalar2=None, op0=ALU.mult),
                reads=["xin0", rk], writes=[xbk])

        def p1_back(ti, tile):
            (src, rows, col0, kind) = tile
            to_featmajor(xn[ti % 2], rows, col0, V_G1, f"xn{ti % 2}")

        def phase1(g, skip=0):
            tiles = tiles_of(g)
            switch("T1", T1A)
            for ti, tile in enumerate(tiles):
                if ti < skip:
                    continue
                p1_front(ti, tile)
                p1_back(ti, tile)

        def phase8(g):
            mtiles = [t for t in tiles_of(g) if t[3] != "halo"]
            for ti, (src, rows, col0, kind) in enumerate(mtiles):
                junk = R4[:, :, :].rearrange("p k c -> p (k c)")[:rows, 0:D]
                rs, rk = rms_stats(h[:rows, ti, :], rows, junk, [f"h{ti}"], ["mixT"])
                P.op("dve", lambda e, rows=rows, rs=rs, ti=ti: e.scalar_tensor_tensor(
                    out=h[:rows, ti, :], in0=h[:rows, ti, :], scalar=rs, in1=gfs[:rows, :], op0=ALU.mult, op1=ALU.mult),
                    reads=[f"h{ti}", rk, "gfs"], writes=[f"h{ti}"])
                if kind == "p":
                    t = 4 * g + ti
                    dst = yp[128 * t:128 * (t + 1), :]
                else:
                    dst = ys[:, :]
                P.dma("sp", lambda e, dst=dst, rows=rows, ti=ti: e.dma_start(out=dst, in_=h[:rows, ti, :]),
                      reads=[f"h{ti}"], sem=f"d_y{ti}")


        for g in (glist if glist is not None else range(ngroups)):
            last = (g == NGROUPS - 1)
            first = (g == 0)
            tiles = []
            if first:
                tiles.append((xp[0:128, :], 128, 0, "halo"))
            for i in range(4):
                t = 4 * g + i
                tiles.append((xp[128 * (t + 1):128 * (t + 2), :], 128, MO + 128 * i, "p"))
            if last:
                tiles.append((xs[:, :], 64, MO + 512, "s"))
            nmain = NMAIN if last else 512
            SC = MO + 512
            chunks = [(0, 512)] + ([(512, 64)] if last else [])
            hchunks = ([(-128, 128)] if first else []) + chunks
            mtiles = [t for t in tiles if t[3] != "halo"]

            if g == (glist[0] if glist is not None else 0):
                phase1(g)

            switch("R2", R2A)
            switch("UA", ["uaT"])
            if last:
                switch("T1", R3C)
            if not first:
                P.op("act", lambda e: e.copy(out=kT[:, :, 0:128], in_=ckT[:, :, :]), reads=["ckT"], writes=["kT"])
                P.op("act", lambda e: e.copy(out=vtok[:, 0, :], in_=cvt[:, :]), reads=["cvt"], writes=["vtok"])
                P.op("act", lambda e: e.copy(out=uT[:, :, MO - 32:MO], in_=cuT[:, :, :]), reads=["cuT"],
                     writes=[f"uTc{c}" for c in range(8)])

            def fm_group(s, lhs_fn, nk, c0, n, rkeys, rhs_t=None):
                rt_ = nT if rhs_t is None else rhs_t
                off = MO if rhs_t is None else 0
                return mm_group(lambda b: ps[b][:, 0:n], lhs_fn,
                                lambda kc: rt_[:, kc, off + c0:off + c0 + n], nk, [f"w{s}"] + rkeys)

            def tm_group(s, rows, col0, c_lo, c_hi):
                return mm_group(lambda b: ps[b][:rows, 0:c_hi - c_lo],
                                lambda kc: nT[:, kc, col0:col0 + rows],
                                lambda kc: wsl[s][:, kc, c_lo:c_hi], 16, [f"w{s}"] + nTk)

            for a in range(2):
                def qsrc(t, a=a):
                    prs = []
                    for hh in range(2):
                        for gq in range(4):
                            c_src = a * 512 + hh * 256 + gq * 64
                            c_dst = gq * 128 + hh * 64
                            prs.append((t[:, :, c_dst:c_dst + 64],
                                        w_in[:, c_src:c_src + 64].rearrange("(k p) d -> p k d", p=128)))
                    return prs
                s = load_slab(qsrc)
                for gq in range(4):
                    for (c0, n) in chunks:
                        b = fm_group(s, lambda kc, s=s, gq=gq: wsl[s][:, kc, gq * 128:(gq + 1) * 128], 16, c0, n, nTk)
                        P.op("act", lambda e, b=b, a=a, gq=gq, c0=c0, n=n: e.activation(
                            out=qT[:, a * 4 + gq, c0:c0 + n], in_=ps[b][:, 0:n], func=AF.Copy),
                            reads=[f"ps{b}"], writes=["qT"])
            s = load_slab(slab_cols(w_in, 1024))
            for a in range(2):
                for (c0, n) in hchunks:
                    b = fm_group(s, lambda kc, s=s, a=a: wsl[s][:, kc, a * 128:(a + 1) * 128], 16, c0, n, nTk)
                    P.op("act", lambda e, b=b, a=a, c0=c0, n=n: e.activation(
                        out=kT[:, a, MO + c0:MO + c0 + n], in_=ps[b][:, 0:n], func=AF.Copy),
                        reads=[f"ps{b}"], writes=["kT"])
            import os
            out_tiles = [t for t in tiles if last and ((t[3] == "s" and not os.environ.get('NO_SOUT')) or (t[3] == "p" and t[2] == MO + 384 and not os.environ.get('NO_POUT')))]
            for ti, (src, rows, col0, kind) in enumerate(tiles):
                vt = col0 // 128
                b = tm_group(s, rows, col0, 256, 512)
                P.op("dve", lambda e, b=b, rows=rows, vt=vt: e.tensor_copy(out=vtok[:rows, vt, :], in_=ps[b][:rows, 0:256]),
                     reads=[f"ps{b}"], writes=["vtok"])
                if (src, rows, col0, kind) in out_tiles:
                    P.op("dve", lambda e, b=b, rows=rows: e.tensor_copy(out=stg[:rows, 0:256], in_=ps[b][:rows, 0:256]),
                         reads=[f"ps{b}"], writes=["stg"])
                    b2 = tm_group(s, rows, col0, 0, 256)
                    P.op("act", lambda e, b2=b2, rows=rows: e.activation(out=stg[:rows, 256:512], in_=ps[b2][:rows, 0:256],
                                                                        func=AF.Copy),
                         reads=[f"ps{b2}"], writes=["stg2"])
                    if kind == "p":
                        P.dma("sp", lambda e: e.dma_start(out=vp[:, :], in_=stg[:, 0:256]), reads=["stg"], sem=o2sem())
                        P.dma("sp", lambda e: e.dma_start(out=kp[:, :], in_=stg[:, 256:512]), reads=["stg2"], sem=o2sem())
                    else:
                        for sq_ in range(16):
                            P.dma("sp", lambda e, sq_=sq_: e.dma_start(
                                out=vsn[sq_, 124:128, :], in_=stg[4 * sq_:4 * sq_ + 4, 0:256]), reads=["stg"], sem=o2sem())
                            P.dma("sp", lambda e, sq_=sq_: e.dma_start(
                                out=ksn[sq_, 124:128, :], in_=stg[4 * sq_:4 * sq_ + 4, 256:512]), reads=["stg2"], sem=o2sem())

            for half in range(2):
                s = load_slab(slab_cols(w_in, 1536 + half * 512))
                for m in range(4):
                    for (c0, n) in hchunks:
                        b = fm_group(s, lambda kc, s=s, m=m: wsl[s][:, kc, m * 128:(m + 1) * 128], 16, c0, n, nTk)
                        P.op("act", lambda e, b=b, m=m, c0=c0, n=n: e.activation(
                            out=uaT[:, m, MO + c0:MO + c0 + n], in_=ps[b][:, 0:n], func=AF.Copy, scale=0.5),
                            reads=[f"ps{b}"], writes=["uaT"])
                for oi, (src, rows, col0, kind) in enumerate(out_tiles):
                    b = tm_group(s, rows, col0, 0, 512)
                    P.op("act", lambda e, b=b, rows=rows, oi=oi, half=half: e.activation(
                        out=uat[:rows, oi, half * 512:(half + 1) * 512], in_=ps[b][:rows, :], func=AF.Copy, scale=0.5),
                        reads=[f"ps{b}"], writes=[f"uat{oi}"])
                s = load_slab(slab_cols(w_in, 2560 + half * 512))
                for m in range(4):
                    ch = half * 4 + m
                    for (c0, n) in hchunks:
                        b = fm_group(s, lambda kc, s=s, m=m: wsl[s][:, kc, m * 128:(m + 1) * 128], 16, c0, n, nTk)
                        gi = b % 2
                        P.op("act", lambda e, b=b, n=n, gi=gi: e.activation(
                            out=gt[gi][:, 0:n], in_=ps[b][:, 0:n], func=AF.Tanh, scale=0.5),
                            reads=[f"ps{b}"], writes=[f"gt{gi}"])
                        P.op("dve", lambda e, gi=gi, m=m, ch=ch, c0=c0, n=n: e.scalar_tensor_tensor(
                            out=uT[:, ch, MO + c0:MO + c0 + n], in0=gt[gi][:, 0:n], scalar=1.0,
                            in1=uaT[:, m, MO + c0:MO + c0 + n], op0=ALU.add, op1=ALU.mult),
                            reads=[f"gt{gi}", "uaT"], writes=[f"uTc{ch}"])
                for oi, (src, rows, col0, kind) in enumerate(out_tiles):
                    b = tm_group(s, rows, col0, 0, 512)
                    gi = b % 2
                    P.op("act", lambda e, b=b, rows=rows, gi=gi: e.activation(
                        out=gt[gi][:rows, :], in_=ps[b][:rows, :], func=AF.Tanh, scale=0.5),
                        reads=[f"ps{b}"], writes=[f"gt{gi}"])
                    P.op("dve", lambda e, rows=rows, gi=gi, oi=oi, half=half: e.scalar_tensor_tensor(
                        out=stg[:rows, 0:512], in0=gt[gi][:rows, :], scalar=1.0,
                        in1=uat[:rows, oi, half * 512:(half + 1) * 512], op0=ALU.add, op1=ALU.mult),
                        reads=[f"gt{gi}", f"uat{oi}"], writes=["stg", "stg2"])
                    if kind == "p":
                        P.dma("sp", lambda e, half=half: e.dma_start(
                            out=cp[:, half * 512:(half + 1) * 512], in_=stg[98:128, 0:512]),
                            reads=["stg", "stg2"], sem=o2sem())
                    else:
                        for sq_ in range(16):
                            P.dma("sp", lambda e, half=half, sq_=sq_: e.dma_start(
                                out=csn[sq_, 26:30, half * 512:(half + 1) * 512], in_=stg[4 * sq_:4 * sq_ + 4, 0:512]),
                                reads=["stg", "stg2"], sem=o2sem())

            if debug and debug.get("stage") == 2 and g == debug.get("g", 0):
                break

            switch("R3", R3A)
            pt_i = [0]

            def attn_norm(bO, bD, a, hh, c0, n, nq):
                kv = 2 * a + hh
                ri = bD % 2
                hs = slice(hh * 64, (hh + 1) * 64)
                P.op("dve", lambda e: e.tensor_tensor(
                    out=rden[ri][hs, 0:n].rearrange("p (g q) -> p g q", g=4),
                    in0=ps[bD][hs, 0:n].rearrange("p (g q) -> p g q", g=4),
                    in1=esk[hs, kv * 4:(kv + 1) * 4].unsqueeze(2).broadcast_to([64, 4, nq]), op=ALU.add),
                    reads=[f"ps{bD}", "esk"], writes=[f"rden{ri}"])
                P.op("dve", lambda e: e.reciprocal(out=rden[ri][hs, 0:n], in_=rden[ri][hs, 0:n]),
                     reads=[f"rden{ri}"], writes=[f"rden{ri}"])
                P.op("dve", lambda e: e.tensor_tensor(
                    out=oT[hs, a * 4:(a + 1) * 4, c0:c0 + nq],
                    in0=ps[bO][hs, 0:n].rearrange("p (g q) -> p g q", g=4),
                    in1=rden[ri][hs, 0:n].rearrange("p (g q) -> p g q", g=4), op=ALU.mult),
                    reads=[f"ps{bO}", f"rden{ri}"], writes=["oT"])

            for i in range(4):
                c0 = 128 * i
                for a in range(2):
                    pts = {}
                    for hh in range(2):
                        hs = slice(hh * 64, (hh + 1) * 64)
                        for blk, (kcol, mi) in enumerate([(MO + c0 - 128, (0 if (first and i == 0) else 4)), (MO + c0, 1)]):
                            b = bank()
                            P.op("pe", lambda e, b=b, mi=mi: e.matmul(
                                ps[b][:, :], lhsT=idb[:, :], rhs=msk[:, mi, :], start=True, stop=False),
                                reads=["idb", "msk"], writes=[f"ps{b}"], inc=False)
                            for gq in range(4):
                                P.op("pe", lambda e, b=b, kcol=kcol, gq=gq, hs=hs: e.matmul(
                                    ps[b][:, gq * 128:(gq + 1) * 128], lhsT=kT[hs, a, kcol:kcol + 128],
                                    rhs=qT[hs, a * 4 + gq, c0:c0 + 128], start=False, stop=(gq == 3)),
                                    reads=["kT", "qT"], writes=[f"ps{b}"], inc=(gq == 3))
                            pi = pt_i[0]
                            pt_i[0] = (pi + 1) % NPT
                            P.op("act", lambda e, b=b, pi=pi: e.activation(out=PT[pi][:, :], in_=ps[b][:, :], func=AF.Exp,
                                                                          scale=SCALE),
                                 reads=[f"ps{b}"], writes=[f"PT{pi}"])
                            pts[(hh, blk)] = pi
                    bO = bank()
                    bD = bank()
                    for hh in range(2):
                        hs = slice(hh * 64, (hh + 1) * 64)
                        kv = 2 * a + hh
                        for blk, vt in enumerate([i, i + 1]):
                            pi = pts[(hh, blk)]
                            P.op("pe", lambda e, blk=blk, pi=pi, vt=vt, hs=hs, kv=kv: e.matmul(
                                ps[bO][hs, :], lhsT=vtok[:, vt, kv * 64:(kv + 1) * 64], rhs=PT[pi][:, :],
                                start=(blk == 0), stop=(blk == 1)), reads=["vtok", f"PT{pi}"], writes=[f"ps{bO}"],
                                inc=(blk == 1))
                        for blk in range(2):
                            pi = pts[(hh, blk)]
                            P.op("pe", lambda e, blk=blk, pi=pi, hs=hs: e.matmul(
                                ps[bD][hs, :], lhsT=onesb[:, 0:64], rhs=PT[pi][:, :],
                                start=(blk == 0), stop=(blk == 1)), reads=["onesb", f"PT{pi}"], writes=[f"ps{bD}"],
                                inc=(blk == 1))
                    ri = bD % 2
                    P.op("dve", lambda e, ri=ri, bD=bD: e.tensor_tensor(
                        out=rden[ri][:, :].rearrange("p (g q) -> p g q", g=4),
                        in0=ps[bD][:, :].rearrange("p (g q) -> p g q", g=4),
                        in1=esk2[:, a, :].unsqueeze(2).broadcast_to([128, 4, 128]), op=ALU.add),
                        reads=[f"ps{bD}", "esk2"], writes=[f"rden{ri}"])
                    P.op("dve", lambda e, ri=ri: e.reciprocal(out=rden[ri][:, :], in_=rden[ri][:, :]),
                         reads=[f"rden{ri}"], writes=[f"rden{ri}"])
                    P.op("dve", lambda e, ri=ri, bO=bO: e.tensor_tensor(
                        out=oT[:, a * 4:(a + 1) * 4, c0:c0 + 128],
                        in0=ps[bO][:, :].rearrange("p (g q) -> p g q", g=4),
                        in1=rden[ri][:, :].rearrange("p (g q) -> p g q", g=4), op=ALU.mult),
                        reads=[f"ps{bO}", f"rden{ri}"], writes=["oT"])

            if last and not os.environ.get('NO_SATT'):
                switch("R4", R4B)
                P.dma("pool", lambda e: e.dma_start(out=vcb[:, :, :], in_=cv.rearrange("b w c -> w b c")),
                      writes=["vcb"], sem="d_vcb")
                for q in range(4):
                    P.dma("pool", lambda e, q=q: e.dma_start(
                        out=kcb[:, :, :], in_=ck[4 * q:4 * q + 4, :, :].rearrange("b w c -> w b c")),
                        writes=["kcb"], sem="d_kcb")
                    for a in range(2):
                        b = bank()
                        for bb in range(4):
                            P.op("pe", lambda e, b=b, bb=bb, a=a: e.matmul(
                                ps[b][:, bb * 128:(bb + 1) * 128], lhsT=kcb[:, bb, a * 128:(a + 1) * 128], rhs=idb[:, :],
                                start=True, stop=True), reads=["kcb", "idb"], writes=[f"ps{b}"], inc=(bb == 3))
                        P.op("act", lambda e, b=b, a=a, q=q: e.activation(
                            out=KcT[:, a, 4 * q:4 * q + 4, :], in_=ps[b][:, :].rearrange("p (b w) -> p b w", b=4),
                            func=AF.Copy), reads=[f"ps{b}"], writes=["KcT"])
                LV = int(os.environ.get('SATT_LEVEL', '9'))
                for a in range(2 if LV >= 2 else 0):
                    P.op("pool", lambda e, a=a: e.tensor_copy(
                        out=qS[:, a, :].rearrange("p (b g t) -> p b g t", b=16, g=4),
                        in_=qT[:, a * 4:(a + 1) * 4, 512:576].rearrange("p g (b t) -> p b g t", t=4)),
                        reads=["qT"], writes=["qS"])
                for a in range(2 if LV >= 3 else 0):
                    for hh in range(2):
                        hs = slice(hh * 64, (hh + 1) * 64)
                        kv = 2 * a + hh
                        bn = bank()
                        SKN = bool(os.environ.get('SKIP_NEW')); SKC = bool(os.environ.get('SKIP_CACHE'))
                        pn = pt_i[0]
                        pt_i[0] = (pn + 1) % NPT
                        if not SKN:
                            P.op("pe", lambda e, bn=bn: e.matmul(
                                ps[bn][0:64, 0:256], lhsT=idb[:, 0:64], rhs=msk[:, 3, 0:256], start=True, stop=False),
                                reads=["idb", "msk"], writes=[f"ps{bn}"], inc=False)
                            P.op("pe", lambda e, bn=bn: e.matmul(
                                ps[bn][0:64, 0:256], lhsT=kT[hs, a, SC:SC + 64], rhs=qS[hs, a, 0:256],
                                start=False, stop=True), reads=["kT", "qS"], writes=[f"ps{bn}"])
                            pass
                            pass
                            P.op("act", lambda e, bn=bn, pn=pn: e.activation(
                                out=PT[pn][0:64, 0:256], in_=ps[bn][0:64, 0:256], func=AF.Exp, scale=SCALE),
                                reads=[f"ps{bn}"], writes=[f"PT{pn}"])
                        bc = bank()
                        if not SKC:
                            P.op("pe", lambda e, bc=bc: e.matmul(
                                ps[bc][:, 0:256], lhsT=idb[:, :], rhs=msk[:, 2, 0:256], start=True, stop=False),
                                reads=["idb", "msk"], writes=[f"ps{bc}"], inc=False)
                            for sq_ in range(16):
                                P.op("pe", lambda e, bc=bc, sq_=sq_: e.matmul(
                                    ps[bc][:, 16 * sq_:16 * sq_ + 16], lhsT=KcT[hs, a, sq_, :],
                                    rhs=qS[hs, a, 16 * sq_:16 * sq_ + 16],
                                    start=False, stop=(sq_ == 15)), reads=["KcT", "qS"], writes=[f"ps{bc}"], inc=(sq_ == 15))
                        pc = pt_i[0]
                        pt_i[0] = (pc + 1) % NPT
                        P.op("act", lambda e, bc=bc, pc=pc: e.activation(
                            out=PT[pc][:, 0:256], in_=ps[bc][:, 0:256], func=AF.Exp, scale=SCALE),
                            reads=[f"ps{bc}"], writes=[f"PT{pc}"])
                        if LV < 4:
                            continue
                        bO = bank()
                        bD = bank()
                        P.op("pe", lambda e, bO=bO, pn=pn: e.matmul(
                            ps[bO][:, 0:256], lhsT=vtok[0:64, 5, a * 128:(a + 1) * 128], rhs=PT[pn][0:64, 0:256],
                            start=True, stop=False), reads=["vtok", f"PT{pn}"], writes=[f"ps{bO}"], inc=False)
                        for sq_ in range(16):
                            P.op("pe", lambda e, bO=bO, pc=pc, sq_=sq_: e.matmul(
                                ps[bO][:, 16 * sq_:16 * sq_ + 16], lhsT=vcb[:, sq_, a * 128:(a + 1) * 128],
                                rhs=PT[pc][:, 16 * sq_:16 * sq_ + 16],
                                start=False, stop=(sq_ == 15)), reads=["vcb", f"PT{pc}"], writes=[f"ps{bO}"],
                                inc=(sq_ == 15))
                        P.op("pe", lambda e, bD=bD, pn=pn: e.matmul(
                            ps[bD][:, 0:256], lhsT=onesb[0:64, :], rhs=PT[pn][0:64, 0:256], start=True, stop=False),
                            reads=["onesb", f"PT{pn}"], writes=[f"ps{bD}"], inc=False)
                        P.op("pe", lambda e, bD=bD, pc=pc: e.matmul(
                            ps[bD][:, 0:256], lhsT=onesb[:, :], rhs=PT[pc][:, 0:256], start=False, stop=True),
                            reads=["onesb", f"PT{pc}"], writes=[f"ps{bD}"])
                        if LV < 5:
                            continue
                        ri = bD % 2
                        for gq in range(4):
                            hd = kv * 4 + gq
                            P.op("dve", lambda e, gq=gq, hd=hd, ri=ri, bD=bD: e.tensor_scalar(
                                out=rden[ri][hs, 0:256].rearrange("p (b g t) -> p g b t", b=16, g=4)[:, gq, :, :],
                                in0=ps[bD][hs, 0:256].rearrange("p (b g t) -> p g b t", b=16, g=4)[:, gq, :, :],
                                scalar1=esk[hs, hd:hd + 1], scalar2=None, op0=ALU.add),
                                reads=[f"ps{bD}", "esk"], writes=[f"rden{ri}"])
                        P.op("dve", lambda e, ri=ri: e.reciprocal(out=rden[ri][hs, 0:256], in_=rden[ri][hs, 0:256]),
                             reads=[f"rden{ri}"], writes=[f"rden{ri}"])
                        P.op("dve", lambda e, ri=ri, bO=bO: e.tensor_tensor(
                            out=oT[hs, a * 4:(a + 1) * 4, 512:576].rearrange("p g (b t) -> p g b t", t=4),
                            in0=ps[bO][hs, 0:256].rearrange("p (b g t) -> p g b t", b=16, g=4),
                            in1=rden[ri][hs, 0:256].rearrange("p (b g t) -> p g b t", b=16, g=4), op=ALU.mult),
                            reads=[f"ps{bO}", f"rden{ri}"], writes=["oT"])

            if last and not os.environ.get('NO_SCONV'):
                switch("R4", R4C)
                for q in range(4):
                    P.dma("pool", lambda e, q=q: e.dma_start(
                        out=stb[0:120, q, :], in_=st[4 * q:4 * q + 4, :, :].rearrange("b r c -> (b r) c")),
                        writes=["stb"], sem=f"d_stb{q}", merge=(q > 0))
                for q in range(4):
                    for c4 in range(2):
                        b = bank()
                        for j in range(4):
                            ch = c4 * 4 + j
                            P.op("pe", lambda e, b=b, j=j, ch=ch, q=q: e.matmul(
                                ps[b][:, j * 128:j * 128 + 120], lhsT=stb[0:120, q, ch * 128:(ch + 1) * 128],
                                rhs=idb[0:120, 0:120], start=True, stop=True),
                                reads=["stb", "idb"], writes=[f"ps{b}"], inc=(j == 3))
                        P.op("act", lambda e, b=b, c4=c4, q=q: e.activation(
                            out=ubT[:, c4 * 4:(c4 + 1) * 4, 4 * q:4 * q + 4, 0:30],
                            in_=ps[b][:, :].rearrange("p (j c) -> p j c", j=4)[:, :, 0:120].rearrange(
                                "p j (b r) -> p j b r", b=4),
                            func=AF.Copy), reads=[f"ps{b}"], writes=["ubT"])
                P.op("pool", lambda e: e.tensor_copy(
                    out=ubT[:, :, :, 30:34], in_=uT[:, :, SC:SC + 64].rearrange("p k (b t) -> p k b t", t=4)),
                    reads=[f"uTc{c}" for c in range(8)], writes=["ubT"])

            if not last:
                P.op("act", lambda e: e.copy(out=ckT[:, :, :], in_=kT[:, :, MO + 384:MO + 512]),
                     reads=["kT"], writes=["ckT"])
                P.op("act", lambda e: e.copy(out=cvt[:, :], in_=vtok[:, 4, :]), reads=["vtok"], writes=["cvt"])
                P.op("act", lambda e: e.copy(out=cuT[:, :, :], in_=uT[:, :, MO + 480:MO + 512]),
                     reads=[f"uTc{c}" for c in range(8)], writes=["cuT"])

            switch("T1", T1B)
            switch("UA", ["dgA", "dgB"])
            bM = [bank() for _ in chunks]
            bQ = [bank() for _ in chunks]
            reserved.update(bM + bQ)
            for pr in range(4):
                chs = [2 * pr, 2 * pr + 1]
                for ci, ch in enumerate(chs):
                    A = acc[ci]
                    bC = bank()
                    for (j0, j1, dk) in [(0, 16, "dgA"), (16, 31, "dgB")]:
                        nj = j1 - j0
                        P.op("dve" if j0 == 0 else "pool", lambda e, ch=ch, j0=j0, j1=j1, nj=nj: e.tensor_tensor(
                            out=dg[:, j0:j1, :], in0=idb[:, :].unsqueeze(1).broadcast_to([128, nj, 128]),
                            in1=vec[:, V_CW + ch * 31 + j0:V_CW + ch * 31 + j1].unsqueeze(2).broadcast_to([128, nj, 128]),
                            op=ALU.mult), reads=["idb", "vec"], writes=[dk])
                        for j in range(j0, j1):
                            P.op("pe", lambda e, ch=ch, j=j, bC=bC: e.matmul(
                                ps[bC][:, :], lhsT=dg[:, j, :], rhs=uT[:, ch, MO - 30 + j:MO - 30 + j + 512],
                                start=(j == 0), stop=(j == 30)), reads=[dk, f"uTc{ch}"], writes=[f"ps{bC}"],
                                inc=(j == j1 - 1))
                    P.op("act", lambda e, A=A, ch=ch, bC=bC: e.activation(
                        out=A[:, 0:512], in_=ps[bC][:, :], func=AF.Identity, bias=vec[:, V_CB + ch:V_CB + ch + 1]),
                        reads=[f"ps{bC}", "vec"], writes=[f"acc{ci}"])
                    if last:
                        ubf = ubT[:, ch, :, :].rearrange("p b r -> p (b r)")
                        bS = [bank(), bank()]
                        for hf in range(2):
                            for j in range(31):
                                P.op("pe", lambda e, hf=hf, j=j: e.matmul(
                                    ps[bS[hf]][:, 0:257], lhsT=dg[:, j, :], rhs=ubf[:, 257 * hf + j:257 * hf + j + 257],
                                    start=(j == 0), stop=(j == 30)), reads=["dgA" if j < 16 else "dgB", "ubT"],
                                    writes=[f"ps{bS[hf]}"], inc=(j == 15 or j == 30))
                            o0 = 0 if hf == 0 else 15
                            P.op("act", lambda e, hf=hf, o0=o0, A=A, ch=ch: e.activation(
                                out=A[:, 512 + 32 * hf:544 + 32 * hf].rearrange("p (b t) -> p b t", t=4),
                                in_=ps[bS[hf]][:, o0:o0 + 272].rearrange("p (b r) -> p b r", r=34)[:, :, 0:4],
                                func=AF.Identity, bias=vec[:, V_CB + ch:V_CB + ch + 1]),
                                reads=[f"ps{bS[hf]}", "vec"], writes=[f"acc{ci}s"])
                for ci, ch in enumerate(chs):
                    A = acc[ci]
                    B = sq[ci]
                    ak = [f"acc{ci}"] + ([f"acc{ci}s"] if last else [])
                    P.op("act", lambda e, A=A, B=B: e.activation(out=B[:, 0:nmain], in_=A[:, 0:nmain], func=AF.Square),
                         reads=ak, writes=[f"sq{ci}"])
                    for xi, (c0, n) in enumerate(chunks):
                        P.op("pe", lambda e, A=A, xi=xi, c0=c0, n=n, ch=ch: e.matmul(
                            ps[bM[xi]][:, 0:n], lhsT=onesf[:, :], rhs=A[:, c0:c0 + n], start=(ch == 0), stop=(ch == 7)),
                            reads=ak + ["onesf"], writes=[f"ps{bM[xi]}"])
                        P.op("pe", lambda e, B=B, xi=xi, c0=c0, n=n, ch=ch: e.matmul(
                            ps[bQ[xi]][:, 0:n], lhsT=onesf[:, :], rhs=B[:, c0:c0 + n], start=(ch == 0), stop=(ch == 7)),
                            reads=[f"sq{ci}", "onesf"], writes=[f"ps{bQ[xi]}"])
                    P.op("act", lambda e, A=A, ch=ch: e.activation(out=cT[:, ch, 0:nmain], in_=A[:, 0:nmain], func=AF.Copy),
                         reads=ak, writes=[f"cTc{ch}"])
            for xi, (c0, n) in enumerate(chunks):
                P.op("act", lambda e, xi=xi, c0=c0, n=n: e.activation(out=lnm[:, c0:c0 + n], in_=ps[bM[xi]][:, 0:n],
                                                                     func=AF.Copy),
                     reads=[f"ps{bM[xi]}"], writes=["lnm"])
                P.op("dve", lambda e, c0=c0, n=n: e.tensor_tensor(out=lnt[:, c0:c0 + n], in0=lnm[:, c0:c0 + n],
                                                                  in1=lnm[:, c0:c0 + n], op=ALU.mult),
                     reads=["lnm"], writes=["lnt"])
                P.op("dve", lambda e, xi=xi, c0=c0, n=n: e.tensor_tensor(out=lnr[:, c0:c0 + n], in0=ps[bQ[xi]][:, 0:n],
                                                                        in1=lnt[:, c0:c0 + n], op=ALU.subtract),
                     reads=[f"ps{bQ[xi]}", "lnt"], writes=["lnr"])
                P.op("act", lambda e, c0=c0, n=n: e.activation(out=lnr[:, c0:c0 + n], in_=lnr[:, c0:c0 + n],
                                                               func=AF.Sqrt, bias=EPS),
                     reads=["lnr"], writes=["lnr"])
                P.op("dve", lambda e, c0=c0, n=n: e.reciprocal(out=lnr[:, c0:c0 + n], in_=lnr[:, c0:c0 + n]),
                     reads=["lnr"], writes=["lnr"])
            reserved.clear()
            for ch in range(8):
                ci = ch % 2
                A = acc[ci]
                B = sq[ci]
                P.op("dve", lambda e, A=A, ch=ch: e.tensor_tensor(out=A[:, 0:nmain], in0=cT[:, ch, 0:nmain],
                                                                  in1=lnm[:, 0:nmain], op=ALU.subtract),
                     reads=[f"cTc{ch}", "lnm"], writes=[f"acc{ci}", f"acc{ci}s"])
                P.op("dve", lambda e, A=A: e.tensor_tensor(out=A[:, 0:nmain], in0=A[:, 0:nmain], in1=lnr[:, 0:nmain],
                                                           op=ALU.mult),
                     reads=[f"acc{ci}", "lnr"], writes=[f"acc{ci}"])
                P.op("act", lambda e, A=A, B=B, ch=ch: e.activation(
                    out=B[:, 0:nmain], in_=A[:, 0:nmain], func=AF.Identity, scale=hvec[:, ch:ch + 1],
                    bias=hvec[:, 8 + ch:9 + ch]), reads=[f"acc{ci}", "hvec"], writes=[f"sq{ci}"])
                P.op("act", lambda e, A=A, B=B: e.activation(out=A[:, 0:nmain], in_=B[:, 0:nmain], func=AF.Tanh),
                     reads=[f"sq{ci}"], writes=[f"acc{ci}"])
                P.op("dve", lambda e, A=A, B=B, ch=ch: e.scalar_tensor_tensor(
                    out=sT[:, ch, 0:nmain], in0=A[:, 0:nmain], scalar=1.0, in1=B[:, 0:nmain], op0=ALU.add, op1=ALU.mult),
                    reads=[f"acc{ci}", f"sq{ci}"], writes=["sT"])

            if debug and debug.get("stage") == 3 and g == debug.get("g", 0):
                break

            switch("R4", R4A)
            switch("UA", [f"tA{m_}" for m_ in range(4)])
            switch("T1", [f"G1{m_}" for m_ in range(4)])
            G1 = T1[:, 0:4 * NMAIN].rearrange("p (m c) -> p m c", c=NMAIN)
            for cb in range(4):
                sA = load_slab(slab_cols(w_in, 3584 + cb * 512))
                for m in range(4):
                    jj = cb * 4 + m
                    for (c0, n) in chunks:
                        b0 = fm_group(sA, lambda kc, m=m: wsl[sA][:, kc, m * 128:(m + 1) * 128], 16, c0, n, nTk)
                        P.op("act", lambda e, b0=b0, n=n, m=m, c0=c0, jj=jj: e.activation(
                            out=tmpA[:, m, c0:c0 + n], in_=ps[b0][:, 0:n], func=AF.Tanh, scale=0.5, bias=hbg[:, jj:jj + 1]),
                            reads=[f"ps{b0}", "hbg"], writes=[f"tA{m}"])
                sB = load_slab(slab_wo(cb))
                for m in range(4):
                    for (c0, n) in chunks:
                        b1 = fm_group(sB, lambda kc, m=m: wsl[sB][:, kc, m * 128:(m + 1) * 128], 8, c0, n, ["oT"], rhs_t=oT)
                        P.op("dve", lambda e, b1=b1, m=m, c0=c0, n=n: e.scalar_tensor_tensor(
                            out=tmpA[:, m, c0:c0 + n], in0=tmpA[:, m, c0:c0 + n], scalar=1.0, in1=ps[b1][:, 0:n],
                            op0=ALU.add, op1=ALU.mult), reads=[f"tA{m}", f"ps{b1}"], writes=[f"tA{m}"])
                sD = load_slab(slab_cols(w_in, 5632 + cb * 512))
                for m in range(4):
                    jj = cb * 4 + m
                    for (c0, n) in chunks:
                        b3 = fm_group(sD, lambda kc, m=m: wsl[sD][:, kc, m * 128:(m + 1) * 128], 16, c0, n, nTk)
                        P.op("act", lambda e, b3=b3, n=n, m=m, c0=c0, jj=jj: e.activation(
                            out=G1[:, m, c0:c0 + n], in_=ps[b3][:, 0:n], func=AF.Tanh, scale=0.5,
                            bias=hbg[:, 16 + jj:17 + jj]), reads=[f"ps{b3}", "hbg"], writes=[f"G1{m}"])
                sC = load_slab(slab_cols(w_co, cb * 512, nk=8))
                for m in range(4):
                    jj = cb * 4 + m
                    for (c0, n) in chunks:
                        b2 = fm_group(sC, lambda kc, m=m: wsl[sC][:, kc, m * 128:(m + 1) * 128], 8, c0, n, ["sT"], rhs_t=sT)
                        P.op("dve", lambda e, b2=b2, m=m, c0=c0, n=n: e.scalar_tensor_tensor(
                            out=G1[:, m, c0:c0 + n], in0=G1[:, m, c0:c0 + n], scalar=1.0, in1=ps[b2][:, 0:n],
                            op0=ALU.add, op1=ALU.mult), reads=[f"G1{m}", f"ps{b2}"], writes=[f"G1{m}"])
                        P.op("dve", lambda e, m=m, jj=jj, c0=c0, n=n: e.tensor_tensor(
                            out=mixT[:, jj, c0:c0 + n], in0=G1[:, m, c0:c0 + n], in1=tmpA[:, m, c0:c0 + n], op=ALU.add),
                            reads=[f"G1{m}", f"tA{m}"], writes=["mixT"])

            switch("R2", R2H)
            for ti, (src, rows, col0, kind) in enumerate(mtiles):
                P.dma("sp", lambda e, src=src, rows=rows, ti=ti: e.dma_start(out=h[:rows, ti, :], in_=src),
                      writes=[f"h{ti}"], sem=f"d_h{ti}")
            def n2_front(ti):
                (src, rows, col0, kind) = mtiles[ti]
                xb = xn[ti % 2]
                xbk = f"xn{ti % 2}"
                rs, rk = rms_stats(h[:rows, ti, :], rows, xb[:rows, :], [f"h{ti}"], [xbk])
                P.op("dve", lambda e: e.tensor_scalar(
                    out=xb[:rows, :], in0=h[:rows, ti, :], scalar1=rs, scalar2=None, op0=ALU.mult),
                    reads=[f"h{ti}", rk], writes=[xbk])

            def n2_back(ti):
                (src, rows, col0, kind) = mtiles[ti]
                to_featmajor(xn[ti % 2], rows, col0, V_G2, f"xn{ti % 2}")

            for cb in range(4):
                s = load_slab(slab_cols(w_out, cb * 512))
                if cb == 3:
                    switch("T1", T1A)
                for ti, (src, rows, col0, kind) in enumerate(mtiles):
                    b = mm_group(lambda b, rows=rows: ps[b][:rows, :],
                                 lambda kc, ti=ti, rows=rows: mixT[:, kc, ti * 128:ti * 128 + rows],
                                 lambda kc, s=s: wsl[s][:, kc, :], 16, [f"w{s}", "mixT"])
                    P.op("dve", lambda e, b=b, rows=rows, ti=ti, cb=cb: e.scalar_tensor_tensor(
                        out=h[:rows, ti, cb * 512:(cb + 1) * 512], in0=ps[b][:rows, :], scalar=0.5,
                        in1=h[:rows, ti, cb * 512:(cb + 1) * 512], op0=ALU.mult, op1=ALU.add),
                        reads=[f"ps{b}", f"h{ti}"], writes=[f"h{ti}"])
                    if cb == 3:
                        if ti >= 2:
                            n2_back(ti - 2)
                        n2_front(ti)
            for ti in range(max(0, len(mtiles) - 2), len(mtiles)):
                n2_back(ti)

            gl = list(glist if glist is not None else range(ngroups))
            nxt = gl[gl.index(g) + 1] if gl.index(g) + 1 < len(gl) else None
            nxt_tiles = tiles_of(nxt) if nxt is not None else []
            switch("R3", R3B)
            for fb in range(4):
                for sl in range(4):
                    s = load_slab(slab_cols(w_up, fb * 2048 + sl * 512))
                    for m in range(4):
                        for (c0, n) in chunks:
                            b = fm_group(s, lambda kc, s=s, m=m: wsl[s][:, kc, m * 128:(m + 1) * 128], 16, c0, n, nTk)
                            ri = b % 2
                            P.op("act", lambda e, b=b, n=n, ri=ri: e.activation(
                                out=rt[ri][:, 0:n], in_=ps[b][:, 0:n], func=AF.Relu),
                                reads=[f"ps{b}"], writes=[f"rt{ri}"])
                            P.op("act", lambda e, ri=ri, sl=sl, m=m, c0=c0, n=n: e.activation(
                                out=a2T[:, sl * 4 + m, c0:c0 + n], in_=rt[ri][:, 0:n], func=AF.Square),
                                reads=[f"rt{ri}"], writes=["a2T"])
                for cb in range(4):
                    pre = (fb == 3 and nxt is not None)
                    if pre:
                        p1_front(cb, nxt_tiles[cb])
                    s = load_slab(slab_cols(w_dn, cb * 512, r0=fb * 2048))
                    for ti, (src, rows, col0, kind) in enumerate(mtiles):
                        b = mm_group(lambda b, rows=rows: ps[b][:rows, :],
                                     lambda kc, ti=ti, rows=rows: a2T[:, kc, ti * 128:ti * 128 + rows],
                                     lambda kc, s=s: wsl[s][:, kc, :], 16, [f"w{s}", "a2T"])
                        P.op("dve", lambda e, b=b, rows=rows, ti=ti, cb=cb: e.tensor_tensor(
                            out=h[:rows, ti, cb * 512:(cb + 1) * 512], in0=ps[b][:rows, :],
                            in1=h[:rows, ti, cb * 512:(cb + 1) * 512], op=ALU.add),
                            reads=[f"ps{b}", f"h{ti}"], writes=[f"h{ti}"])
                    if pre:
                        p1_back(cb, nxt_tiles[cb])

            if nxt is not None:
                phase1(nxt, skip=4)
            phase8(g)

        if debug:
            def dump(name, ap, shape, dt, keys):
                d = dout("dbg_" + name, shape, dt)
                P.dma("sp", lambda e: e.dma_start(out=d, in_=ap), reads=keys, sem="d_dbg")
            dump("nT", nT[:, :, :], [128, 16, NCOL], BF16, nTk)
            dump("R2", R2[:, :], [128, 5 * D], F32, R2A + R2H)
            dump("R3", R3[:, :, :], [128, 16, NMAIN], BF16, R3A + R3B + R3C)
            dump("R4", R4[:, :, :], [128, 16, NMAIN], BF16, R4A + R4B + R4C)
        P.emit()
    return nc


def _prep_inputs(x_prompt, x_sample, cache_k, cache_v, state_conv, norm1_g, w_in, b_gate, sink,
                 w_attn_o, conv_w, conv_b, cln_g, cln_b, w_conv_o, w_out, norm2_g, w_up, w_down, norm_f_g):
    f = lambda a: np.ascontiguousarray(np.asarray(a, dtype=np.float32))
    x_prompt, x_sample = f(x_prompt), f(x_sample)
    cache_k, cache_v, state_conv = f(cache_k)[0], f(cache_v)[0], f(state_conv)[0]
    fm = lambda v, nk: np.ascontiguousarray(f(v).reshape(nk, 128).T)
    vecs = np.zeros((128, V_N), np.float32)
    vecs[:, V_G1:V_G1 + 16] = fm(norm1_g[0], 16)
    vecs[:, V_G2:V_G2 + 16] = fm(norm2_g[0], 16)
    vecs[:, V_BG:V_BG + 32] = fm(b_gate[0], 32)
    cw = f(conv_w)[0]
    vecs[:, V_CW:V_CW + 248] = cw.reshape(31, 8, 128).transpose(2, 1, 0).reshape(128, 248)
    vecs[:, V_CB:V_CB + 8] = fm(conv_b[0], 8)
    vecs[:, V_LG:V_LG + 8] = fm(cln_g[0], 8)
    vecs[:, V_LB:V_LB + 8] = fm(cln_b[0], 8)
    vecs[:, V_SK:V_SK + 16] = np.broadcast_to(f(sink)[0][None, :], (128, 16))
    gfb = np.ascontiguousarray(np.broadcast_to(f(norm_f_g)[None, :], (128, D)))
    ident = np.eye(128, dtype=np.float32)
    j = np.arange(128)[:, None]
    i = np.arange(128)[None, :]
    m_prev = np.where(j > i, 0.0, NEG).astype(np.float32)
    m_own = np.where(j <= i, 0.0, NEG).astype(np.float32)
    t = np.arange(64)[None, :] % 4
    m_cache = np.where(np.arange(128)[:, None] >= t + 1, 0.0, NEG).astype(np.float32)
    kb, kt = np.arange(64)[:, None] // 4, np.arange(64)[:, None] % 4
    col = np.arange(256)[None, :]
    qb, qt = col // 16, col % 4
    m_new = np.full((128, 256), NEG, np.float32)
    m_new[:64] = np.where((kb == qb) & (kt <= qt), 0.0, NEG)
    m_cache = np.where(np.arange(128)[:, None] >= (col % 4) + 1, 0.0, NEG).astype(np.float32)
    common = dict(
        ident=ident, vecs=vecs, gfb=gfb,
        w_in=f(w_in)[0], w_ao=f(w_attn_o)[0], w_co=f(w_conv_o)[0], w_out=f(w_out)[0],
        w_up=f(w_up)[0], w_dn=f(w_down)[0])
    in_maps = []
    for c in range(NCORES):
        b, qd = c // 4, c % 4
        s0 = qd * 2048
        xpc = np.zeros((17 * 128, D), np.float32)
        if qd > 0:
            xpc[0:128] = x_prompt[b, s0 - 128:s0]
        xpc[128:] = x_prompt[b, s0:s0 + 2048]
        mk = np.zeros((128, 5, 512), np.float32)
        mk[:, 0, :] = np.tile(m_prev if qd > 0 else np.full((128, 128), NEG, np.float32), (1, 4))
        mk[:, 1, :] = np.tile(m_own, (1, 4))
        mk[:, 2, 0:256] = m_cache
        mk[:, 3, 0:256] = m_new
        mk[:, 4, :] = np.tile(m_prev, (1, 4))
        d = dict(common)
        d.update(
            xp=xpc, xs=np.ascontiguousarray(x_sample[16 * c:16 * c + 16].reshape(64, D)),
            ck=np.ascontiguousarray(cache_k[16 * c:16 * c + 16].reshape(16, 128, 256)),
            cv=np.ascontiguousarray(cache_v[16 * c:16 * c + 16].reshape(16, 128, 256)),
            st=np.ascontiguousarray(state_conv[16 * c:16 * c + 16]), masks=mk)
        in_maps.append(d)
    return in_maps


def kernel(**inputs):
    in_maps = _prep_inputs(**inputs)
    nc = build()
    res = run_bass_kernel_spmd(nc, in_maps, core_ids=list(range(NCORES)))
    r = res.results
    y_prompt = np.zeros((2, 8192, D), np.float32)
    for c in range(NCORES):
        y_prompt[c // 4, (c % 4) * 2048:(c % 4 + 1) * 2048] = r[c]["yp"]
    y_sample = np.concatenate([r[c]["ys"].reshape(16, 4, D) for c in range(NCORES)], axis=0)
    nkp = np.stack([r[3]["kp"], r[7]["kp"]]).reshape(1, 2, 128, 4, 64)
    nvp = np.stack([r[3]["vp"], r[7]["vp"]]).reshape(1, 2, 128, 4, 64)
    ncp = np.stack([r[3]["cp"], r[7]["cp"]]).reshape(1, 2, 30, 1024)
    nks = np.concatenate([r[c]["ksn"] for c in range(NCORES)], axis=0).reshape(1, 128, 128, 4, 64)
    nvs = np.concatenate([r[c]["vsn"] for c in range(NCORES)], axis=0).reshape(1, 128, 128, 4, 64)
    ncs = np.concatenate([r[c]["csn"] for c in range(NCORES)], axis=0).reshape(1, 128, 30, 1024)
    f = lambda a: np.ascontiguousarray(a, dtype=np.float32)
    return (f(y_prompt), f(y_sample), f(nkp), f(nvp), f(ncp), f(nks), f(nvs), f(ncs))
```

```python
import numpy as np
from contextlib import ExitStack
import concourse.bass as bass
import concourse.mybir as mybir
from concourse.bass_utils import run_bass_kernel_spmd

F32 = mybir.dt.float32
BF16 = mybir.dt.bfloat16
AF = mybir.ActivationFunctionType
ALU = mybir.AluOpType

D = 2048
NKC = 16
IN_W = 7680
DFF = 8192
EPS = 1e-6
NEG = -30000.0
SCALE = 0.125
NCORES = 8
NGROUPS = 4
MO = 128
NCOL = 704
NMAIN = 576
V_G1, V_G2, V_BG, V_CW, V_CB, V_LG, V_LB, V_SK, V_N = 0, 16, 32, 64, 312, 320, 328, 336, 352


class _Rec:
    def __getattr__(self, name):
        def f(*args, **kwargs):
            return (name, args, kwargs)
        return f


class Prog:
    ENG = ["pe", "act", "dve", "pool", "sp"]

    def __init__(self, nc, es):
        self.nc, self.es = nc, es
        self.streams = {e: [] for e in self.ENG}
        self.semh = {}
        self.cnt = {}
        self.seen = {e: {} for e in self.ENG}
        self.res = {}
        for e in self.ENG:
            self._sem("e_" + e)

    def _sem(self, name):
        if name not in self.semh:
            self.semh[name] = self.es.enter_context(self.nc.semaphore(name))
            self.cnt[name] = 0
        return self.semh[name]

    def _r(self, k):
        if k not in self.res:
            self.res[k] = [{}, {}]
        return self.res[k]

    @staticmethod
    def _merge(d, s):
        for k, v in s.items():
            if d.get(k, 0) < v:
                d[k] = v

    def _deps(self, eng, reads, writes):
        deps = {}
        for k in reads:
            self._merge(deps, self._r(k)[0])
        for k in writes:
            r = self._r(k)
            self._merge(deps, r[0])
            self._merge(deps, r[1])
        waits = []
        my = "e_" + eng
        for s, v in deps.items():
            if s == my and eng == "pe":
                continue
            if self.seen[eng].get(s, 0) >= v:
                continue
            self.seen[eng][s] = v
            waits.append((s, v))
        return waits

    def _commit(self, me, reads, writes):
        for k in reads:
            self._merge(self._r(k)[1], me)
        for k in writes:
            r = self._r(k)
            r[0] = dict(me)
            r[1] = {}

    def op(self, eng, fn, reads=(), writes=(), inc=True):
        waits = self._deps(eng, reads, writes)
        my = "e_" + eng
        val = self.cnt[my] + 1
        if inc:
            self.cnt[my] = val
        self._commit({my: val}, reads, writes)
        self.streams[eng].append((waits, fn(_Rec()), (my, 1) if inc else None))

    def dma(self, q, fn, reads=(), writes=(), sem="d_misc", merge=False):
        self._sem(sem)
        waits = self._deps(q, reads, () if merge else writes)
        prev = self.cnt[sem]
        if prev > 0 and self.seen[q].get(sem, 0) < prev:
            self.seen[q][sem] = prev
            waits.append((sem, prev))
        self.cnt[sem] += 16
        me = {sem: self.cnt[sem]}
        if merge:
            for k in reads:
                self._merge(self._r(k)[1], me)
            for k in writes:
                self._merge(self._r(k)[0], me)
        else:
            self._commit(me, reads, writes)
        self.streams[q].append((waits, fn(_Rec()), (sem, 16)))

    def alias(self, new_keys, old_keys):
        deps = {}
        for k in old_keys:
            r = self._r(k)
            self._merge(deps, r[0])
            self._merge(deps, r[1])
        for k in new_keys:
            r = self._r(k)
            self._merge(r[1], deps)

    def emit(self):
        nc = self.nc
        fin = [(s, v) for s, v in self.cnt.items() if v > 0]
        self.streams["sp"].append((fin, None, None))
        with nc.Block() as block:
            def run(engobj, name):
                for waits, fn, inc in self.streams[name]:
                    for s, v in waits:
                        engobj.wait_ge(self.semh[s], v)
                    if fn is None:
                        continue
                    ins = getattr(engobj, fn[0])(*fn[1], **fn[2])
                    if inc is not None:
                        ins.then_inc(self.semh[inc[0]], inc[1])

            @block.tensor
            def _(e):
                run(e, "pe")

            @block.scalar
            def _(e):
                run(e, "act")

            @block.vector
            def _(e):
                run(e, "dve")

            @block.gpsimd
            def _(e):
                run(e, "pool")

            @block.sync
            def _(e):
                run(e, "sp")


def build(debug=False, ngroups=NGROUPS, glist=None):
    nc = bass.Bass("TRN2", target_bir_lowering=False)
    es = ExitStack()
    with es:
        def din(name, shape, dt=F32):
            return nc.dram_tensor(name, list(shape), dt, kind="ExternalInput").ap()

        def dout(name, shape, dt=F32):
            return nc.dram_tensor(name, list(shape), dt, kind="ExternalOutput").ap()

        xp = din("xp", [17 * 128, D])
        xs = din("xs", [64, D])
        ck = din("ck", [16, 128, 256])
        cv = din("cv", [16, 128, 256])
        st = din("st", [16, 30, 1024])
        masks = din("masks", [128, 5, 512])
        ident = din("ident", [128, 128])
        vecs = din("vecs", [128, V_N])
        gfb = din("gfb", [128, D])
        w_in = din("w_in", [D, IN_W])
        w_ao = din("w_ao", [1024, D])
        w_co = din("w_co", [1024, D])
        w_out = din("w_out", [D, D])
        w_up = din("w_up", [D, DFF])
        w_dn = din("w_dn", [DFF, D])

        yp = dout("yp", [2048, D])
        ys = dout("ys", [64, D])
        kp = dout("kp", [128, 256])
        vp = dout("vp", [128, 256])
        cp = dout("cp", [30, 1024])
        ksn = dout("ksn", [16, 128, 256])
        vsn = dout("vsn", [16, 128, 256])
        csn = dout("csn", [16, 30, 1024])

        def sb(name, shape, dt):
            return es.enter_context(nc.sbuf_tensor(name, list(shape), dt))

        nT = sb("nT", [128, 16, NCOL], BF16)
        R2 = sb("R2", [128, 5 * D], F32)
        R3 = sb("R3", [128, 16, NMAIN], BF16)
        R4 = sb("R4", [128, 16, NMAIN], BF16)
        uaT = sb("uaT", [128, 4, NCOL], F32)
        wsl = [sb(f"wsl{i}", [128, 16, 512], BF16) for i in range(3)]
        T1 = sb("T1", [128, 2 * D], F32)
        gfs = sb("gfs", [128, D], F32)
        vec = sb("vec", [128, V_N], F32)
        msk = sb("msk", [128, 5, 512], BF16)
        idb = sb("idb", [128, 128], BF16)
        onesb = sb("onesb", [128, 128], BF16)
        onesf = sb("onesf", [128, 128], F32)
        hvec = sb("hvec", [128, 16], F32)
        hbg = sb("hbg", [128, 32], F32)
        esk = sb("esk", [128, 16], F32)
        stat = sb("stat", [128, 16], F32)
        NPT = 5
        PT = [sb(f"PT{i}", [128, 512], BF16) for i in range(NPT)]
        esk2 = sb("esk2", [128, 2, 4], F32)
        rden = [sb(f"rden{i}", [128, 512], F32) for i in range(2)]
        gt = [sb(f"gt{i}", [128, 512], F32) for i in range(2)]
        rt = [sb(f"rt{i}", [128, 512], BF16) for i in range(2)]
        ckT = sb("ckT", [128, 2, 128], BF16)
        cvt = sb("cvt", [128, 256], BF16)
        cuT = sb("cuT", [128, 8, 32], BF16)
        qS = sb("qS", [128, 2, 256], BF16)

        ps = [es.enter_context(nc.psum_tensor(f"ps{i}", [128, 512], F32)) for i in range(8)]

        h = R2[:, :].rearrange("p (t c) -> p t c", c=D)
        R2b = R2[:, :].bitcast(BF16)
        o_ = 0
        qT = R2b[:, o_:o_ + 8 * NMAIN].rearrange("p (k c) -> p k c", c=NMAIN); o_ += 8 * NMAIN
        kT = R2b[:, o_:o_ + 2 * NCOL].rearrange("p (k c) -> p k c", c=NCOL); o_ += 2 * NCOL
        vtok = R2b[:, o_:o_ + 6 * 256].rearrange("p (t c) -> p t c", c=256); o_ += 6 * 256
        uT = R2b[:, o_:o_ + 8 * NCOL].rearrange("p (k c) -> p k c", c=NCOL); o_ += 8 * NCOL
        cT = R2b[:, o_:o_ + 8 * NMAIN].rearrange("p (k c) -> p k c", c=NMAIN); o_ += 8 * NMAIN
        assert o_ <= 20480, o_
        oT = R3[:, 0:8, :]
        sT = R3[:, 8:16, :]
        a2T = R3
        mixT = R4
        R4f = R4[:, :, :].rearrange("p k c -> p (k c)")
        KcT = R4f[:, 0:4096].rearrange("p (a b w) -> p a b w", a=2, b=16)
        vcb = R4f[:, 4096:8192].rearrange("p (b c) -> p b c", c=256)
        kcb = R4f[:, 8192:9216].rearrange("p (b c) -> p b c", c=256)
        stb = R4f[:, 0:4096].rearrange("p (b c) -> p b c", c=1024)
        ubT = R4f[:, 4096:4096 + 8 * 16 * 34].rearrange("p (k b r) -> p k b r", k=8, b=16)
        tmpA = uaT
        dg = uaT[:, :, :].rearrange("p k c -> p (k c)").bitcast(BF16)[:, 0:31 * 128].rearrange("p (j c) -> p j c", c=128)
        xin0 = T1[:, 0:D]
        T1b = T1[:, D:2 * D].bitcast(BF16)
        xn = [T1b[:, 0:D], T1b[:, D:2 * D]]
        acc = [T1[:, i * NMAIN:(i + 1) * NMAIN] for i in range(2)]
        sq = [T1[:, (2 + i) * NMAIN:(3 + i) * NMAIN] for i in range(2)]
        lnm = T1[:, 4 * NMAIN:5 * NMAIN]
        lnr = T1[:, 5 * NMAIN:6 * NMAIN]
        lnt = T1[:, 6 * NMAIN:7 * NMAIN]
        uat = T1[:, 0:2048].rearrange("p (o c) -> p o c", c=1024)
        stg = T1[:, 2048:3072]

        P = Prog(nc, es)
        roles = {}

        def switch(region, new_keys):
            old = roles.get(region, [])
            P.alias(new_keys, old)
            roles[region] = list(new_keys)

        bank_i = [0]

        reserved = set()

        def bank():
            while True:
                b = bank_i[0]
                bank_i[0] = (b + 1) % 8
                if b not in reserved:
                    return b

        slot_i = [0]

        def load_slab(src_fn):
            s = slot_i[0]
            slot_i[0] = (s + 1) % 3
            for i, (dst, src) in enumerate(src_fn(wsl[s])):
                P.dma("pool", lambda e, dst=dst, src=src: e.dma_start(out=dst, in_=src),
                      reads=[], writes=[f"w{s}"], sem=f"d_w{s}_{i}", merge=(i > 0))
            return s

        def slab_cols(w, c0, nk=16, r0=0):
            src = w[r0:r0 + nk * 128, c0:c0 + 512].rearrange("(k p) c -> p k c", p=128)
            return lambda t: [(t[:, 0:nk, :], src)]

        def slab_wo(cb):
            def f(t):
                prs = []
                for hh in range(2):
                    for a in range(2):
                        src = w_ao[a * 512 + hh * 256:a * 512 + hh * 256 + 256, cb * 512:(cb + 1) * 512]
                        prs.append((t[hh * 64:(hh + 1) * 64, a * 4:(a + 1) * 4, :],
                                    src.rearrange("(g d) n -> d g n", d=64)))
                return prs
            return f

        P.dma("sp", lambda e: e.dma_start(out=vec[:, :], in_=vecs[:, :]), writes=["vec"], sem="d_vec")
        P.dma("sp", lambda e: e.dma_start(out=gfs[:, :], in_=gfb[:, :]), writes=["gfs"], sem="d_gfs")
        P.dma("pool", lambda e: e.dma_start(out=msk[:, :, :], in_=masks[:, :, :]), writes=["msk"], sem="d_msk")
        P.dma("pool", lambda e: e.dma_start(out=idb[:, :], in_=ident[:, :]), writes=["idb"], sem="d_idb")
        P.op("dve", lambda e: e.memset(onesf[:, :], 1.0 / 1024.0), writes=["onesf"])
        P.op("dve", lambda e: e.memset(onesb[:, :], 1.0), writes=["onesb"])
        P.op("dve", lambda e: e.tensor_scalar(out=hvec[:, 0:16], in0=vec[:, V_LG:V_LG + 16], scalar1=0.5, scalar2=None,
                                              op0=ALU.mult), reads=["vec"], writes=["hvec"])
        P.op("dve", lambda e: e.tensor_scalar(out=hbg[:, :], in0=vec[:, V_BG:V_BG + 32], scalar1=0.5, scalar2=None,
                                              op0=ALU.mult), reads=["vec"], writes=["hbg"])
        P.op("act", lambda e: e.activation(out=esk[:, :], in_=vec[:, V_SK:V_SK + 16], func=AF.Exp),
             reads=["vec"], writes=["esk"])
        for a_ in range(2):
            for hh_ in range(2):
                hs_ = slice(hh_ * 64, (hh_ + 1) * 64)
                kv_ = 2 * a_ + hh_
                P.op("dve", lambda e, a_=a_, hs_=hs_, kv_=kv_: e.tensor_copy(out=esk2[hs_, a_, :], in_=esk[hs_, kv_ * 4:(kv_ + 1) * 4]),
                     reads=["esk"], writes=["esk2"])

        P.dma("sp", lambda e: e.dma_start(out=ksn[:, 0:124, :], in_=ck[:, 4:128, :]), sem="d_o1a")
        P.dma("sp", lambda e: e.dma_start(out=vsn[:, 0:124, :], in_=cv[:, 4:128, :]), sem="d_o1b")
        P.dma("sp", lambda e: e.dma_start(out=csn[:, 0:26, :], in_=st[:, 4:30, :]), sem="d_o1c")

        o2_i = [0]

        def o2sem():
            o2_i[0] += 1
            return f"d_o2_{o2_i[0] % 8}"

        stat_i = [0]

        def rms_stats(src_ap, rows, junk_ap, rkeys, jkeys):
            c = stat_i[0]
            stat_i[0] = (c + 1) % 8
            a = stat[:rows, 2 * c:2 * c + 1]
            b = stat[:rows, 2 * c + 1:2 * c + 2]
            k = f"stat{c}"
            P.op("act", lambda e: e.activation(out=junk_ap, in_=src_ap, func=AF.Square, accum_out=a),
                 reads=rkeys, writes=jkeys + [k])
            P.op("act", lambda e: e.activation(out=b, in_=a, func=AF.Sqrt, scale=1.0 / D, bias=EPS),
                 reads=[k], writes=[k + "b"])
            P.op("dve", lambda e: e.reciprocal(out=b, in_=b), reads=[k + "b"], writes=[k + "b"])
            return b, k + "b"

        SQD = float(np.sqrt(D))
        nTk = [f"nT{c}" for c in range(6)]

        def to_featmajor(src_bf, rows, col0, gcol, skey):
            for q4 in range(4):
                b = bank()
                for j in range(4):
                    kc = q4 * 4 + j
                    P.op("pe", lambda e, b=b, j=j, kc=kc: e.matmul(
                        ps[b][:, j * 128:j * 128 + rows], lhsT=src_bf[:rows, kc * 128:(kc + 1) * 128],
                        rhs=idb[:rows, :rows], start=True, stop=True),
                        reads=[skey, "idb"], writes=[f"ps{b}"], inc=(j == 3))
                for j in range(4):
                    kc = q4 * 4 + j
                    P.op("act", lambda e, b=b, j=j, kc=kc: e.activation(
                        out=nT[:, kc, col0:col0 + rows], in_=ps[b][:, j * 128:j * 128 + rows], func=AF.Identity,
                        scale=vec[:, gcol + kc:gcol + kc + 1]),
                        reads=[f"ps{b}", "vec"], writes=[f"nT{col0 // 128}"])

        def mm_group(out_fn, lhs_fn, rhs_fn, nk, rkeys):
            b = bank()
            for kc in range(nk):
                P.op("pe", lambda e, b=b, kc=kc: e.matmul(out_fn(b), lhsT=lhs_fn(kc), rhs=rhs_fn(kc),
                                                          start=(kc == 0), stop=(kc == nk - 1)),
                     reads=rkeys, writes=[f"ps{b}"], inc=(kc == nk - 1))
            return b

        T1A = ["xin0", "xn0", "xn1"]
        T1B = ["acc0", "acc1", "sq0", "sq1", "lnm", "lnr", "lnt"]
        R2A = ["qT", "kT", "vtok", "uT", "cT"] + [f"uTc{c}" for c in range(8)] + [f"cTc{c}" for c in range(8)]
        R2H = [f"h{t}" for t in range(5)]
        R3A = ["oT", "sT"]
        R3B = ["a2T"]
        R3C = ["uat0", "uat1", "stg", "stg2"]
        R4A = ["mixT"]
        R4B = ["KcT", "vcb", "kcb"]
        R4C = ["stb", "ubT"]

        def tiles_of(g):
            tl = []
            if g == 0:
                tl.append((xp[0:128, :], 128, 0, "halo"))
            for i in range(4):
                t = 4 * g + i
                tl.append((xp[128 * (t + 1):128 * (t + 2), :], 128, MO + 128 * i, "p"))
            if g == NGROUPS - 1:
                tl.append((xs[:, :], 64, MO + 512, "s"))
            return tl

        def p1_front(ti, tile):
            (src, rows, col0, kind) = tile
            xb = xn[ti % 2]
            xbk = f"xn{ti % 2}"
            P.dma("sp", lambda e: e.dma_start(out=xin0[:rows, :], in_=src), writes=["xin0"], sem="d_xin0")
            rs, rk = rms_stats(xin0[:rows, :], rows, xb[:rows, :], ["xin0"], [xbk])
            P.op("dve", lambda e: e.tensor_scalar(
                out=xb[:rows, :], in0=xin0[:rows, :], scalar1=rs, scalar2=None, op0=ALU.mult),
                reads=["xin0", rk], writes=[xbk])

        def p1_back(ti, tile):
            (src, rows, col0, kind) = tile
            to_featmajor(xn[ti % 2], rows, col0, V_G1, f"xn{ti % 2}")

        def phase1(g, skip=0):
            tiles = tiles_of(g)
            switch("T1", T1A)
            for ti, tile in enumerate(tiles):
                if ti < skip:
                    continue
                p1_front(ti, tile)
                p1_back(ti, tile)

        def phase8(g):
            mtiles = [t for t in tiles_of(g) if t[3] != "halo"]
            for ti, (src, rows, col0, kind) in enumerate(mtiles):
                junk = R4[:, :, :].rearrange("p k c -> p (k c)")[:rows, 0:D]
                rs, rk = rms_stats(h[:rows, ti, :], rows, junk, [f"h{ti}"], ["mixT"])
                P.op("dve", lambda e, rows=rows, rs=rs, ti=ti: e.scalar_tensor_tensor(
                    out=h[:rows, ti, :], in0=h[:rows, ti, :], scalar=rs, in1=gfs[:rows, :], op0=ALU.mult, op1=ALU.mult),
                    reads=[f"h{ti}", rk, "gfs"], writes=[f"h{ti}"])
                if kind == "p":
                    t = 4 * g + ti
                    dst = yp[128 * t:128 * (t + 1), :]
                else:
                    dst = ys[:, :]
                P.dma("sp", lambda e, dst=dst, rows=rows, ti=ti: e.dma_start(out=dst, in_=h[:rows, ti, :]),
                      reads=[f"h{ti}"], sem=f"d_y{ti}")


        for g in (glist if glist is not None else range(ngroups)):
            last = (g == NGROUPS - 1)
            first = (g == 0)
            tiles = []
            if first:
                tiles.append((xp[0:128, :], 128, 0, "halo"))
            for i in range(4):
                t = 4 * g + i
                tiles.append((xp[128 * (t + 1):128 * (t + 2), :], 128, MO + 128 * i, "p"))
            if last:
                tiles.append((xs[:, :], 64, MO + 512, "s"))
            nmain = NMAIN if last else 512
            SC = MO + 512
            chunks = [(0, 512)] + ([(512, 64)] if last else [])
            hchunks = ([(-128, 128)] if first else []) + chunks
            mtiles = [t for t in tiles if t[3] != "halo"]

            if g == (glist[0] if glist is not None else 0):
                phase1(g)

            switch("R2", R2A)
            switch("UA", ["uaT"])
            if last:
                switch("T1", R3C)
            if not first:
                P.op("act", lambda e: e.copy(out=kT[:, :, 0:128], in_=ckT[:, :, :]), reads=["ckT"], writes=["kT"])
                P.op("act", lambda e: e.copy(out=vtok[:, 0, :], in_=cvt[:, :]), reads=["cvt"], writes=["vtok"])
                P.op("act", lambda e: e.copy(out=uT[:, :, MO - 32:MO], in_=cuT[:, :, :]), reads=["cuT"],
                     writes=[f"uTc{c}" for c in range(8)])

            def fm_group(s, lhs_fn, nk, c0, n, rkeys, rhs_t=None):
                rt_ = nT if rhs_t is None else rhs_t
                off = MO if rhs_t is None else 0
                return mm_group(lambda b: ps[b][:, 0:n], lhs_fn,
                                lambda kc: rt_[:, kc, off + c0:off + c0 + n], nk, [f"w{s}"] + rkeys)

            def tm_group(s, rows, col0, c_lo, c_hi):
                return mm_group(lambda b: ps[b][:rows, 0:c_hi - c_lo],
                                lambda kc: nT[:, kc, col0:col0 + rows],
                                lambda kc: wsl[s][:, kc, c_lo:c_hi], 16, [f"w{s}"] + nTk)

            for a in range(2):
                s = load_slab(slab_cols(w_in, a * 512))
                for gq in range(4):
                    for (c0, n) in chunks:
                        b = bank()
                        for hh in range(2):
                            hs = slice(hh * 64, (hh + 1) * 64)
                            cc = hh * 256 + gq * 64
                            for kc in range(16):
                                P.op("pe", lambda e, b=b, kc=kc, hs=hs, cc=cc, s=s: e.matmul(
                                    ps[b][hs, 0:n], lhsT=wsl[s][:, kc, cc:cc + 64], rhs=nT[:, kc, MO + c0:MO + c0 + n],
                                    start=(kc == 0), stop=(kc == 15)),
                                    reads=[f"w{s}"] + nTk, writes=[f"ps{b}"], inc=(kc == 15))
                        P.op("act", lambda e, b=b, a=a, gq=gq, c0=c0, n=n: e.activation(
                            out=qT[:, a * 4 + gq, c0:c0 + n], in_=ps[b][:, 0:n], func=AF.Copy),
                            reads=[f"ps{b}"], writes=["qT"])
            s = load_slab(slab_cols(w_in, 1024))
            for a in range(2):
                for (c0, n) in hchunks:
                    b = fm_group(s, lambda kc, s=s, a=a: wsl[s][:, kc, a * 128:(a + 1) * 128], 16, c0, n, nTk)
                    P.op("act", lambda e, b=b, a=a, c0=c0, n=n: e.activation(
                        out=kT[:, a, MO + c0:MO + c0 + n], in_=ps[b][:, 0:n], func=AF.Copy),
                        reads=[f"ps{b}"], writes=["kT"])
            out_tiles = [t for t in tiles if last and ((t[3] == "s") or (t[3] == "p" and t[2] == MO + 384))]
            for ti, (src, rows, col0, kind) in enumerate(tiles):
                vt = col0 // 128
                b = tm_group(s, rows, col0, 256, 512)
                P.op("dve", lambda e, b=b, rows=rows, vt=vt: e.tensor_copy(out=vtok[:rows, vt, :], in_=ps[b][:rows, 0:256]),
                     reads=[f"ps{b}"], writes=["vtok"])
                if (src, rows, col0, kind) in out_tiles:
                    P.op("dve", lambda e, b=b, rows=rows: e.tensor_copy(out=stg[:rows, 0:256], in_=ps[b][:rows, 0:256]),
                         reads=[f"ps{b}"], writes=["stg"])
                    b2 = tm_group(s, rows, col0, 0, 256)
                    P.op("act", lambda e, b2=b2, rows=rows: e.activation(out=stg[:rows, 256:512], in_=ps[b2][:rows, 0:256],
                                                                        func=AF.Copy),
                         reads=[f"ps{b2}"], writes=["stg2"])
                    if kind == "p":
                        P.dma("sp", lambda e: e.dma_start(out=vp[:, :], in_=stg[:, 0:256]), reads=["stg"], sem=o2sem())
                        P.dma("sp", lambda e: e.dma_start(out=kp[:, :], in_=stg[:, 256:512]), reads=["stg2"], sem=o2sem())
                    else:
                        for sq_ in range(16):
                            P.dma("sp", lambda e, sq_=sq_: e.dma_start(
                                out=vsn[sq_, 124:128, :], in_=stg[4 * sq_:4 * sq_ + 4, 0:256]), reads=["stg"], sem=o2sem())
                            P.dma("sp", lambda e, sq_=sq_: e.dma_start(
                                out=ksn[sq_, 124:128, :], in_=stg[4 * sq_:4 * sq_ + 4, 256:512]), reads=["stg2"], sem=o2sem())

            for half in range(2):
                s = load_slab(slab_cols(w_in, 1536 + half * 512))
                for m in range(4):
                    for (c0, n) in hchunks:
                        b = fm_group(s, lambda kc, s=s, m=m: wsl[s][:, kc, m * 128:(m + 1) * 128], 16, c0, n, nTk)
                        P.op("act", lambda e, b=b, m=m, c0=c0, n=n: e.activation(
                            out=uaT[:, m, MO + c0:MO + c0 + n], in_=ps[b][:, 0:n], func=AF.Copy, scale=0.5),
                            reads=[f"ps{b}"], writes=["uaT"])
                for oi, (src, rows, col0, kind) in enumerate(out_tiles):
                    b = tm_group(s, rows, col0, 0, 512)
                    P.op("act", lambda e, b=b, rows=rows, oi=oi, half=half: e.activation(
                        out=uat[:rows, oi, half * 512:(half + 1) * 512], in_=ps[b][:rows, :], func=AF.Copy, scale=0.5),
                        reads=[f"ps{b}"], writes=[f"uat{oi}"])
                s = load_slab(slab_cols(w_in, 2560 + half * 512))
                for m in range(4):
                    ch = half * 4 + m
                    for (c0, n) in hchunks:
                        b = fm_group(s, lambda kc, s=s, m=m: wsl[s][:, kc, m * 128:(m + 1) * 128], 16, c0, n, nTk)
                        gi = b % 2
                        P.op("act", lambda e, b=b, n=n, gi=gi: e.activation(
                            out=gt[gi][:, 0:n], in_=ps[b][:, 0:n], func=AF.Tanh, scale=0.5),
                            reads=[f"ps{b}"], writes=[f"gt{gi}"])
                        P.op("dve", lambda e, gi=gi, m=m, ch=ch, c0=c0, n=n: e.scalar_tensor_tensor(
                            out=uT[:, ch, MO + c0:MO + c0 + n], in0=gt[gi][:, 0:n], scalar=1.0,
                            in1=uaT[:, m, MO + c0:MO + c0 + n], op0=ALU.add, op1=ALU.mult),
                            reads=[f"gt{gi}", "uaT"], writes=[f"uTc{ch}"])
                for oi, (src, rows, col0, kind) in enumerate(out_tiles):
                    b = tm_group(s, rows, col0, 0, 512)
                    gi = b % 2
                    P.op("act", lambda e, b=b, rows=rows, gi=gi: e.activation(
                        out=gt[gi][:rows, :], in_=ps[b][:rows, :], func=AF.Tanh, scale=0.5),
                        reads=[f"ps{b}"], writes=[f"gt{gi}"])
                    P.op("dve", lambda e, rows=rows, gi=gi, oi=oi, half=half: e.scalar_tensor_tensor(
                        out=stg[:rows, 0:512], in0=gt[gi][:rows, :], scalar=1.0,
                        in1=uat[:rows, oi, half * 512:(half + 1) * 512], op0=ALU.add, op1=ALU.mult),
                        reads=[f"gt{gi}", f"uat{oi}"], writes=["stg", "stg2"])
                    if kind == "p":
                        P.dma("sp", lambda e, half=half: e.dma_start(
                            out=cp[:, half * 512:(half + 1) * 512], in_=stg[98:128, 0:512]),
                            reads=["stg", "stg2"], sem=o2sem())
                    else:
                        for sq_ in range(16):
                            P.dma("sp", lambda e, half=half, sq_=sq_: e.dma_start(
                                out=csn[sq_, 26:30, half * 512:(half + 1) * 512], in_=stg[4 * sq_:4 * sq_ + 4, 0:512]),
                                reads=["stg", "stg2"], sem=o2sem())

            if debug and debug.get("stage") == 2 and g == debug.get("g", 0):
                break

            switch("R3", R3A)
            pt_i = [0]

            def attn_norm(bO, bD, a, hh, c0, n, nq):
                kv = 2 * a + hh
                ri = bD % 2
                hs = slice(hh * 64, (hh + 1) * 64)
                P.op("dve", lambda e: e.tensor_tensor(
                    out=rden[ri][hs, 0:n].rearrange("p (g q) -> p g q", g=4),
                    in0=ps[bD][hs, 0:n].rearrange("p (g q) -> p g q", g=4),
                    in1=esk[hs, kv * 4:(kv + 1) * 4].unsqueeze(2).broadcast_to([64, 4, nq]), op=ALU.add),
                    reads=[f"ps{bD}", "esk"], writes=[f"rden{ri}"])
                P.op("dve", lambda e: e.reciprocal(out=rden[ri][hs, 0:n], in_=rden[ri][hs, 0:n]),
                     reads=[f"rden{ri}"], writes=[f"rden{ri}"])
                P.op("dve", lambda e: e.tensor_tensor(
                    out=oT[hs, a * 4:(a + 1) * 4, c0:c0 + nq],
                    in0=ps[bO][hs, 0:n].rearrange("p (g q) -> p g q", g=4),
                    in1=rden[ri][hs, 0:n].rearrange("p (g q) -> p g q", g=4), op=ALU.mult),
                    reads=[f"ps{bO}", f"rden{ri}"], writes=["oT"])

            for i in range(4):
                c0 = 128 * i
                for a in range(2):
                    pts = {}
                    for hh in range(2):
                        hs = slice(hh * 64, (hh + 1) * 64)
                        for blk, (kcol, mi) in enumerate([(MO + c0 - 128, (0 if (first and i == 0) else 4)), (MO + c0, 1)]):
                            b = bank()
                            P.op("pe", lambda e, b=b, mi=mi: e.matmul(
                                ps[b][:, :], lhsT=idb[:, :], rhs=msk[:, mi, :], start=True, stop=False),
                                reads=["idb", "msk"], writes=[f"ps{b}"], inc=False)
                            for gq in range(4):
                                P.op("pe", lambda e, b=b, kcol=kcol, gq=gq, hs=hs: e.matmul(
                                    ps[b][:, gq * 128:(gq + 1) * 128], lhsT=kT[hs, a, kcol:kcol + 128],
                                    rhs=qT[hs, a * 4 + gq, c0:c0 + 128], start=False, stop=(gq == 3)),
                                    reads=["kT", "qT"], writes=[f"ps{b}"], inc=(gq == 3))
                            pi = pt_i[0]
                            pt_i[0] = (pi + 1) % NPT
                            P.op("act", lambda e, b=b, pi=pi: e.activation(out=PT[pi][:, :], in_=ps[b][:, :], func=AF.Exp,
                                                                          scale=SCALE),
                                 reads=[f"ps{b}"], writes=[f"PT{pi}"])
                            pts[(hh, blk)] = pi
                    bO = bank()
                    bD = bank()
                    for hh in range(2):
                        hs = slice(hh * 64, (hh + 1) * 64)
                        kv = 2 * a + hh
                        for blk, vt in enumerate([i, i + 1]):
                            pi = pts[(hh, blk)]
                            P.op("pe", lambda e, blk=blk, pi=pi, vt=vt, hs=hs, kv=kv: e.matmul(
                                ps[bO][hs, :], lhsT=vtok[:, vt, kv * 64:(kv + 1) * 64], rhs=PT[pi][:, :],
                                start=(blk == 0), stop=(blk == 1)), reads=["vtok", f"PT{pi}"], writes=[f"ps{bO}"],
                                inc=(blk == 1))
                        for blk in range(2):
                            pi = pts[(hh, blk)]
                            P.op("pe", lambda e, blk=blk, pi=pi, hs=hs: e.matmul(
                                ps[bD][hs, :], lhsT=onesb[:, 0:64], rhs=PT[pi][:, :],
                                start=(blk == 0), stop=(blk == 1)), reads=["onesb", f"PT{pi}"], writes=[f"ps{bD}"],
                                inc=(blk == 1))
                    ri = bD % 2
                    P.op("dve", lambda e, ri=ri, bD=bD: e.tensor_tensor(
                        out=rden[ri][:, :].rearrange("p (g q) -> p g q", g=4),
                        in0=ps[bD][:, :].rearrange("p (g q) -> p g q", g=4),
                        in1=esk2[:, a, :].unsqueeze(2).broadcast_to([128, 4, 128]), op=ALU.add),
                        reads=[f"ps{bD}", "esk2"], writes=[f"rden{ri}"])
                    P.op("dve", lambda e, ri=ri: e.reciprocal(out=rden[ri][:, :], in_=rden[ri][:, :]),
                         reads=[f"rden{ri}"], writes=[f"rden{ri}"])
                    P.op("dve", lambda e, ri=ri, bO=bO: e.tensor_tensor(
                        out=oT[:, a * 4:(a + 1) * 4, c0:c0 + 128],
                        in0=ps[bO][:, :].rearrange("p (g q) -> p g q", g=4),
                        in1=rden[ri][:, :].rearrange("p (g q) -> p g q", g=4), op=ALU.mult),
                        reads=[f"ps{bO}", f"rden{ri}"], writes=["oT"])

            if last:
                switch("R4", R4B)
                P.dma("pool", lambda e: e.dma_start(out=vcb[:, :, :], in_=cv.rearrange("b w c -> w b c")),
                      writes=["vcb"], sem="d_vcb")
                for q in range(4):
                    P.dma("pool", lambda e, q=q: e.dma_start(
                        out=kcb[:, :, :], in_=ck[4 * q:4 * q + 4, :, :].rearrange("b w c -> w b c")),
                        writes=["kcb"], sem="d_kcb")
                    for a in range(2):
                        b = bank()
                        for bb in range(4):
                            P.op("pe", lambda e, b=b, bb=bb, a=a: e.matmul(
                                ps[b][:, bb * 128:(bb + 1) * 128], lhsT=kcb[:, bb, a * 128:(a + 1) * 128], rhs=idb[:, :],
                                start=True, stop=True), reads=["kcb", "idb"], writes=[f"ps{b}"], inc=(bb == 3))
                        P.op("act", lambda e, b=b, a=a, q=q: e.activation(
                            out=KcT[:, a, 4 * q:4 * q + 4, :], in_=ps[b][:, :].rearrange("p (b w) -> p b w", b=4),
                            func=AF.Copy), reads=[f"ps{b}"], writes=["KcT"])
                for a in range(2):
                    P.op("pool", lambda e, a=a: e.tensor_copy(
                        out=qS[:, a, :].rearrange("p (b g t) -> p b g t", b=16, g=4),
                        in_=qT[:, a * 4:(a + 1) * 4, 512:576].rearrange("p g (b t) -> p b g t", t=4)),
                        reads=["qT"], writes=["qS"])
                for a in range(2):
                    for hh in range(2):
                        hs = slice(hh * 64, (hh + 1) * 64)
                        kv = 2 * a + hh
                        bn = bank()
                        pn = pt_i[0]
                        pt_i[0] = (pn + 1) % NPT
                        if True:
                            P.op("pe", lambda e, bn=bn: e.matmul(
                                ps[bn][0:64, 0:256], lhsT=idb[:, 0:64], rhs=msk[:, 3, 0:256], start=True, stop=False),
                                reads=["idb", "msk"], writes=[f"ps{bn}"], inc=False)
                            P.op("pe", lambda e, bn=bn: e.matmul(
                                ps[bn][0:64, 0:256], lhsT=kT[hs, a, SC:SC + 64], rhs=qS[hs, a, 0:256],
                                start=False, stop=True), reads=["kT", "qS"], writes=[f"ps{bn}"])
                            pass
                            pass
                            P.op("act", lambda e, bn=bn, pn=pn: e.activation(
                                out=PT[pn][0:64, 0:256], in_=ps[bn][0:64, 0:256], func=AF.Exp, scale=SCALE),
                                reads=[f"ps{bn}"], writes=[f"PT{pn}"])
                        bc = bank()
                        if True:
                            P.op("pe", lambda e, bc=bc: e.matmul(
                                ps[bc][:, 0:256], lhsT=idb[:, :], rhs=msk[:, 2, 0:256], start=True, stop=False),
                                reads=["idb", "msk"], writes=[f"ps{bc}"], inc=False)
                            for sq_ in range(16):
                                P.op("pe", lambda e, bc=bc, sq_=sq_: e.matmul(
                                    ps[bc][:, 16 * sq_:16 * sq_ + 16], lhsT=KcT[hs, a, sq_, :],
                                    rhs=qS[hs, a, 16 * sq_:16 * sq_ + 16],
                                    start=False, stop=(sq_ == 15)), reads=["KcT", "qS"], writes=[f"ps{bc}"], inc=(sq_ == 15))
                        pc = pt_i[0]
                        pt_i[0] = (pc + 1) % NPT
                        P.op("act", lambda e, bc=bc, pc=pc: e.activation(
                            out=PT[pc][:, 0:256], in_=ps[bc][:, 0:256], func=AF.Exp, scale=SCALE),
                            reads=[f"ps{bc}"], writes=[f"PT{pc}"])
                        bO = bank()
                        bD = bank()
                        P.op("pe", lambda e, bO=bO, pn=pn: e.matmul(
                            ps[bO][:, 0:256], lhsT=vtok[0:64, 5, a * 128:(a + 1) * 128], rhs=PT[pn][0:64, 0:256],
                            start=True, stop=False), reads=["vtok", f"PT{pn}"], writes=[f"ps{bO}"], inc=False)
                        for sq_ in range(16):
                            P.op("pe", lambda e, bO=bO, pc=pc, sq_=sq_: e.matmul(
                                ps[bO][:, 16 * sq_:16 * sq_ + 16], lhsT=vcb[:, sq_, a * 128:(a + 1) * 128],
                                rhs=PT[pc][:, 16 * sq_:16 * sq_ + 16],
                                start=False, stop=(sq_ == 15)), reads=["vcb", f"PT{pc}"], writes=[f"ps{bO}"],
                                inc=(sq_ == 15))
                        P.op("pe", lambda e, bD=bD, pn=pn: e.matmul(
                            ps[bD][:, 0:256], lhsT=onesb[0:64, :], rhs=PT[pn][0:64, 0:256], start=True, stop=False),
                            reads=["onesb", f"PT{pn}"], writes=[f"ps{bD}"], inc=False)
                        P.op("pe", lambda e, bD=bD, pc=pc: e.matmul(
                            ps[bD][:, 0:256], lhsT=onesb[:, :], rhs=PT[pc][:, 0:256], start=False, stop=True),
                            reads=["onesb", f"PT{pc}"], writes=[f"ps{bD}"])
                        ri = bD % 2
                        for gq in range(4):
                            hd = kv * 4 + gq
                            P.op("dve", lambda e, gq=gq, hd=hd, ri=ri, bD=bD: e.tensor_scalar(
                                out=rden[ri][hs, 0:256].rearrange("p (b g t) -> p g b t", b=16, g=4)[:, gq, :, :],
                                in0=ps[bD][hs, 0:256].rearrange("p (b g t) -> p g b t", b=16, g=4)[:, gq, :, :],
                                scalar1=esk[hs, hd:hd + 1], scalar2=None, op0=ALU.add),
                                reads=[f"ps{bD}", "esk"], writes=[f"rden{ri}"])
                        P.op("dve", lambda e, ri=ri: e.reciprocal(out=rden[ri][hs, 0:256], in_=rden[ri][hs, 0:256]),
                             reads=[f"rden{ri}"], writes=[f"rden{ri}"])
                        P.op("dve", lambda e, ri=ri, bO=bO: e.tensor_tensor(
                            out=oT[hs, a * 4:(a + 1) * 4, 512:576].rearrange("p g (b t) -> p g b t", t=4),
                            in0=ps[bO][hs, 0:256].rearrange("p (b g t) -> p g b t", b=16, g=4),
                            in1=rden[ri][hs, 0:256].rearrange("p (b g t) -> p g b t", b=16, g=4), op=ALU.mult),
                            reads=[f"ps{bO}", f"rden{ri}"], writes=["oT"])

            if last:
                switch("R4", R4C)
                for q in range(4):
                    P.dma("pool", lambda e, q=q: e.dma_start(
                        out=stb[0:120, q, :], in_=st[4 * q:4 * q + 4, :, :].rearrange("b r c -> (b r) c")),
                        writes=["stb"], sem=f"d_stb{q}", merge=(q > 0))
                for q in range(4):
                    for c4 in range(2):
                        b = bank()
                        for j in range(4):
                            ch = c4 * 4 + j
                            P.op("pe", lambda e, b=b, j=j, ch=ch, q=q: e.matmul(
                                ps[b][:, j * 128:j * 128 + 120], lhsT=stb[0:120, q, ch * 128:(ch + 1) * 128],
                                rhs=idb[0:120, 0:120], start=True, stop=True),
                                reads=["stb", "idb"], writes=[f"ps{b}"], inc=(j == 3))
                        P.op("act", lambda e, b=b, c4=c4, q=q: e.activation(
                            out=ubT[:, c4 * 4:(c4 + 1) * 4, 4 * q:4 * q + 4, 0:30],
                            in_=ps[b][:, :].rearrange("p (j c) -> p j c", j=4)[:, :, 0:120].rearrange(
                                "p j (b r) -> p j b r", b=4),
                            func=AF.Copy), reads=[f"ps{b}"], writes=["ubT"])
                P.op("pool", lambda e: e.tensor_copy(
                    out=ubT[:, :, :, 30:34], in_=uT[:, :, SC:SC + 64].rearrange("p k (b t) -> p k b t", t=4)),
                    reads=[f"uTc{c}" for c in range(8)], writes=["ubT"])

            if not last:
                P.op("act", lambda e: e.copy(out=ckT[:, :, :], in_=kT[:, :, MO + 384:MO + 512]),
                     reads=["kT"], writes=["ckT"])
                P.op("act", lambda e: e.copy(out=cvt[:, :], in_=vtok[:, 4, :]), reads=["vtok"], writes=["cvt"])
                P.op("act", lambda e: e.copy(out=cuT[:, :, :], in_=uT[:, :, MO + 480:MO + 512]),
                     reads=[f"uTc{c}" for c in range(8)], writes=["cuT"])

            switch("T1", T1B)
            switch("UA", ["dgA", "dgB"])
            bM = [bank() for _ in chunks]
            bQ = [bank() for _ in chunks]
            reserved.update(bM + bQ)
            for pr in range(4):
                chs = [2 * pr, 2 * pr + 1]
                for ci, ch in enumerate(chs):
                    A = acc[ci]
                    bC = bank()
                    for (j0, j1, dk) in [(0, 16, "dgA"), (16, 31, "dgB")]:
                        nj = j1 - j0
                        P.op("dve" if j0 == 0 else "pool", lambda e, ch=ch, j0=j0, j1=j1, nj=nj: e.tensor_tensor(
                            out=dg[:, j0:j1, :], in0=idb[:, :].unsqueeze(1).broadcast_to([128, nj, 128]),
                            in1=vec[:, V_CW + ch * 31 + j0:V_CW + ch * 31 + j1].unsqueeze(2).broadcast_to([128, nj, 128]),
                            op=ALU.mult), reads=["idb", "vec"], writes=[dk])
                        for j in range(j0, j1):
                            P.op("pe", lambda e, ch=ch, j=j, bC=bC: e.matmul(
                                ps[bC][:, :], lhsT=dg[:, j, :], rhs=uT[:, ch, MO - 30 + j:MO - 30 + j + 512],
                                start=(j == 0), stop=(j == 30)), reads=[dk, f"uTc{ch}"], writes=[f"ps{bC}"],
                                inc=(j == j1 - 1))
                    P.op("act", lambda e, A=A, ch=ch, bC=bC: e.activation(
                        out=A[:, 0:512], in_=ps[bC][:, :], func=AF.Identity, bias=vec[:, V_CB + ch:V_CB + ch + 1]),
                        reads=[f"ps{bC}", "vec"], writes=[f"acc{ci}"])
                    if last:
                        ubf = ubT[:, ch, :, :].rearrange("p b r -> p (b r)")
                        bS = [bank(), bank()]
                        for hf in range(2):
                            for j in range(31):
                                P.op("pe", lambda e, hf=hf, j=j: e.matmul(
                                    ps[bS[hf]][:, 0:257], lhsT=dg[:, j, :], rhs=ubf[:, 257 * hf + j:257 * hf + j + 257],
                                    start=(j == 0), stop=(j == 30)), reads=["dgA" if j < 16 else "dgB", "ubT"],
                                    writes=[f"ps{bS[hf]}"], inc=(j == 15 or j == 30))
                            o0 = 0 if hf == 0 else 15
                            P.op("act", lambda e, hf=hf, o0=o0, A=A, ch=ch: e.activation(
                                out=A[:, 512 + 32 * hf:544 + 32 * hf].rearrange("p (b t) -> p b t", t=4),
                                in_=ps[bS[hf]][:, o0:o0 + 272].rearrange("p (b r) -> p b r", r=34)[:, :, 0:4],
                                func=AF.Identity, bias=vec[:, V_CB + ch:V_CB + ch + 1]),
                                reads=[f"ps{bS[hf]}", "vec"], writes=[f"acc{ci}s"])
                for ci, ch in enumerate(chs):
                    A = acc[ci]
                    B = sq[ci]
                    ak = [f"acc{ci}"] + ([f"acc{ci}s"] if last else [])
                    P.op("act", lambda e, A=A, B=B: e.activation(out=B[:, 0:nmain], in_=A[:, 0:nmain], func=AF.Square),
                         reads=ak, writes=[f"sq{ci}"])
                    for xi, (c0, n) in enumerate(chunks):
                        P.op("pe", lambda e, A=A, xi=xi, c0=c0, n=n, ch=ch: e.matmul(
                            ps[bM[xi]][:, 0:n], lhsT=onesf[:, :], rhs=A[:, c0:c0 + n], start=(ch == 0), stop=(ch == 7)),
                            reads=ak + ["onesf"], writes=[f"ps{bM[xi]}"])
                        P.op("pe", lambda e, B=B, xi=xi, c0=c0, n=n, ch=ch: e.matmul(
                            ps[bQ[xi]][:, 0:n], lhsT=onesf[:, :], rhs=B[:, c0:c0 + n], start=(ch == 0), stop=(ch == 7)),
                            reads=[f"sq{ci}", "onesf"], writes=[f"ps{bQ[xi]}"])
                    P.op("act", lambda e, A=A, ch=ch: e.activation(out=cT[:, ch, 0:nmain], in_=A[:, 0:nmain], func=AF.Copy),
                         reads=ak, writes=[f"cTc{ch}"])
            for xi, (c0, n) in enumerate(chunks):
                P.op("act", lambda e, xi=xi, c0=c0, n=n: e.activation(out=lnm[:, c0:c0 + n], in_=ps[bM[xi]][:, 0:n],
                                                                     func=AF.Copy),
                     reads=[f"ps{bM[xi]}"], writes=["lnm"])
                P.op("dve", lambda e, c0=c0, n=n: e.tensor_tensor(out=lnt[:, c0:c0 + n], in0=lnm[:, c0:c0 + n],
                                                                  in1=lnm[:, c0:c0 + n], op=ALU.mult),
                     reads=["lnm"], writes=["lnt"])
                P.op("dve", lambda e, xi=xi, c0=c0, n=n: e.tensor_tensor(out=lnr[:, c0:c0 + n], in0=ps[bQ[xi]][:, 0:n],
                                                                        in1=lnt[:, c0:c0 + n], op=ALU.subtract),
                     reads=[f"ps{bQ[xi]}", "lnt"], writes=["lnr"])
                P.op("act", lambda e, c0=c0, n=n: e.activation(out=lnr[:, c0:c0 + n], in_=lnr[:, c0:c0 + n],
                                                               func=AF.Sqrt, bias=EPS),
                     reads=["lnr"], writes=["lnr"])
                P.op("dve", lambda e, c0=c0, n=n: e.reciprocal(out=lnr[:, c0:c0 + n], in_=lnr[:, c0:c0 + n]),
                     reads=["lnr"], writes=["lnr"])
            reserved.clear()
            for ch in range(8):
                ci = ch % 2
                A = acc[ci]
                B = sq[ci]
                P.op("dve", lambda e, A=A, ch=ch: e.tensor_tensor(out=A[:, 0:nmain], in0=cT[:, ch, 0:nmain],
                                                                  in1=lnm[:, 0:nmain], op=ALU.subtract),
                     reads=[f"cTc{ch}", "lnm"], writes=[f"acc{ci}", f"acc{ci}s"])
                P.op("dve", lambda e, A=A: e.tensor_tensor(out=A[:, 0:nmain], in0=A[:, 0:nmain], in1=lnr[:, 0:nmain],
                                                           op=ALU.mult),
                     reads=[f"acc{ci}", "lnr"], writes=[f"acc{ci}"])
                P.op("act", lambda e, A=A, B=B, ch=ch: e.activation(
                    out=B[:, 0:nmain], in_=A[:, 0:nmain], func=AF.Identity, scale=hvec[:, ch:ch + 1],
                    bias=hvec[:, 8 + ch:9 + ch]), reads=[f"acc{ci}", "hvec"], writes=[f"sq{ci}"])
                P.op("act", lambda e, A=A, B=B: e.activation(out=A[:, 0:nmain], in_=B[:, 0:nmain], func=AF.Tanh),
                     reads=[f"sq{ci}"], writes=[f"acc{ci}"])
                P.op("dve", lambda e, A=A, B=B, ch=ch: e.scalar_tensor_tensor(
                    out=sT[:, ch, 0:nmain], in0=A[:, 0:nmain], scalar=1.0, in1=B[:, 0:nmain], op0=ALU.add, op1=ALU.mult),
                    reads=[f"acc{ci}", f"sq{ci}"], writes=["sT"])

            if debug and debug.get("stage") == 3 and g == debug.get("g", 0):
                break

            switch("R4", R4A)
            switch("UA", [f"tA{m_}" for m_ in range(4)])
            switch("T1", [f"G1{m_}" for m_ in range(4)])
            G1 = T1[:, 0:4 * NMAIN].rearrange("p (m c) -> p m c", c=NMAIN)
            for cb in range(4):
                sA = load_slab(slab_cols(w_in, 3584 + cb * 512))
                for m in range(4):
                    jj = cb * 4 + m
                    for (c0, n) in chunks:
                        b0 = fm_group(sA, lambda kc, m=m: wsl[sA][:, kc, m * 128:(m + 1) * 128], 16, c0, n, nTk)
                        P.op("act", lambda e, b0=b0, n=n, m=m, c0=c0, jj=jj: e.activation(
                            out=tmpA[:, m, c0:c0 + n], in_=ps[b0][:, 0:n], func=AF.Tanh, scale=0.5, bias=hbg[:, jj:jj + 1]),
                            reads=[f"ps{b0}", "hbg"], writes=[f"tA{m}"])
                sB = load_slab(slab_wo(cb))
                for m in range(4):
                    for (c0, n) in chunks:
                        b1 = fm_group(sB, lambda kc, m=m: wsl[sB][:, kc, m * 128:(m + 1) * 128], 8, c0, n, ["oT"], rhs_t=oT)
                        P.op("dve", lambda e, b1=b1, m=m, c0=c0, n=n: e.scalar_tensor_tensor(
                            out=tmpA[:, m, c0:c0 + n], in0=tmpA[:, m, c0:c0 + n], scalar=1.0, in1=ps[b1][:, 0:n],
                            op0=ALU.add, op1=ALU.mult), reads=[f"tA{m}", f"ps{b1}"], writes=[f"tA{m}"])
                sD = load_slab(slab_cols(w_in, 5632 + cb * 512))
                for m in range(4):
                    jj = cb * 4 + m
                    for (c0, n) in chunks:
                        b3 = fm_group(sD, lambda kc, m=m: wsl[sD][:, kc, m * 128:(m + 1) * 128], 16, c0, n, nTk)
                        P.op("act", lambda e, b3=b3, n=n, m=m, c0=c0, jj=jj: e.activation(
                            out=G1[:, m, c0:c0 + n], in_=ps[b3][:, 0:n], func=AF.Tanh, scale=0.5,
                            bias=hbg[:, 16 + jj:17 + jj]), reads=[f"ps{b3}", "hbg"], writes=[f"G1{m}"])
                sC = load_slab(slab_cols(w_co, cb * 512, nk=8))
                for m in range(4):
                    jj = cb * 4 + m
                    for (c0, n) in chunks:
                        b2 = fm_group(sC, lambda kc, m=m: wsl[sC][:, kc, m * 128:(m + 1) * 128], 8, c0, n, ["sT"], rhs_t=sT)
                        P.op("dve", lambda e, b2=b2, m=m, c0=c0, n=n: e.scalar_tensor_tensor(
                            out=G1[:, m, c0:c0 + n], in0=G1[:, m, c0:c0 + n], scalar=1.0, in1=ps[b2][:, 0:n],
                            op0=ALU.add, op1=ALU.mult), reads=[f"G1{m}", f"ps{b2}"], writes=[f"G1{m}"])
                        P.op("dve", lambda e, m=m, jj=jj, c0=c0, n=n: e.tensor_tensor(
                            out=mixT[:, jj, c0:c0 + n], in0=G1[:, m, c0:c0 + n], in1=tmpA[:, m, c0:c0 + n], op=ALU.add),
                            reads=[f"G1{m}", f"tA{m}"], writes=["mixT"])

            switch("R2", R2H)
            for ti, (src, rows, col0, kind) in enumerate(mtiles):
                P.dma("sp", lambda e, src=src, rows=rows, ti=ti: e.dma_start(out=h[:rows, ti, :], in_=src),
                      writes=[f"h{ti}"], sem=f"d_h{ti}")
            def n2_front(ti):
                (src, rows, col0, kind) = mtiles[ti]
                xb = xn[ti % 2]
                xbk = f"xn{ti % 2}"
                rs, rk = rms_stats(h[:rows, ti, :], rows, xb[:rows, :], [f"h{ti}"], [xbk])
                P.op("dve", lambda e: e.tensor_scalar(
                    out=xb[:rows, :], in0=h[:rows, ti, :], scalar1=rs, scalar2=None, op0=ALU.mult),
                    reads=[f"h{ti}", rk], writes=[xbk])

            def n2_back(ti):
                (src, rows, col0, kind) = mtiles[ti]
                to_featmajor(xn[ti % 2], rows, col0, V_G2, f"xn{ti % 2}")

            for cb in range(4):
                s = load_slab(slab_cols(w_out, cb * 512))
                if cb == 3:
                    switch("T1", T1A)
                for ti, (src, rows, col0, kind) in enumerate(mtiles):
                    b = mm_group(lambda b, rows=rows: ps[b][:rows, :],
                                 lambda kc, ti=ti, rows=rows: mixT[:, kc, ti * 128:ti * 128 + rows],
                                 lambda kc, s=s: wsl[s][:, kc, :], 16, [f"w{s}", "mixT"])
                    P.op("dve", lambda e, b=b, rows=rows, ti=ti, cb=cb: e.scalar_tensor_tensor(
                        out=h[:rows, ti, cb * 512:(cb + 1) * 512], in0=ps[b][:rows, :], scalar=0.5,
                        in1=h[:rows, ti, cb * 512:(cb + 1) * 512], op0=ALU.mult, op1=ALU.add),
                        reads=[f"ps{b}", f"h{ti}"], writes=[f"h{ti}"])
                    if cb == 3:
                        if ti >= 2:
                            n2_back(ti - 2)
                        n2_front(ti)
            for ti in range(max(0, len(mtiles) - 2), len(mtiles)):
                n2_back(ti)

            gl = list(glist if glist is not None else range(ngroups))
            nxt = gl[gl.index(g) + 1] if gl.index(g) + 1 < len(gl) else None
            nxt_tiles = tiles_of(nxt) if nxt is not None else []
            switch("R3", R3B)
            for fb in range(4):
                for sl in range(4):
                    s = load_slab(slab_cols(w_up, fb * 2048 + sl * 512))
                    for m in range(4):
                        for (c0, n) in chunks:
                            b = fm_group(s, lambda kc, s=s, m=m: wsl[s][:, kc, m * 128:(m + 1) * 128], 16, c0, n, nTk)
                            ri = b % 2
                            P.op("act", lambda e, b=b, n=n, ri=ri: e.activation(
                                out=rt[ri][:, 0:n], in_=ps[b][:, 0:n], func=AF.Relu),
                                reads=[f"ps{b}"], writes=[f"rt{ri}"])
                            P.op("act", lambda e, ri=ri, sl=sl, m=m, c0=c0, n=n: e.activation(
                                out=a2T[:, sl * 4 + m, c0:c0 + n], in_=rt[ri][:, 0:n], func=AF.Square),
                                reads=[f"rt{ri}"], writes=["a2T"])
                for cb in range(4):
                    pre = (fb == 3 and nxt is not None)
                    if pre:
                        p1_front(cb, nxt_tiles[cb])
                    s = load_slab(slab_cols(w_dn, cb * 512, r0=fb * 2048))
                    for ti, (src, rows, col0, kind) in enumerate(mtiles):
                        b = mm_group(lambda b, rows=rows: ps[b][:rows, :],
                                     lambda kc, ti=ti, rows=rows: a2T[:, kc, ti * 128:ti * 128 + rows],
                                     lambda kc, s=s: wsl[s][:, kc, :], 16, [f"w{s}", "a2T"])
                        P.op("dve", lambda e, b=b, rows=rows, ti=ti, cb=cb: e.tensor_tensor(
                            out=h[:rows, ti, cb * 512:(cb + 1) * 512], in0=ps[b][:rows, :],
                            in1=h[:rows, ti, cb * 512:(cb + 1) * 512], op=ALU.add),
                            reads=[f"ps{b}", f"h{ti}"], writes=[f"h{ti}"])
                    if pre:
                        p1_back(cb, nxt_tiles[cb])

            if nxt is not None:
                phase1(nxt, skip=4)
            phase8(g)

        if debug:
            def dump(name, ap, shape, dt, keys):
                d = dout("dbg_" + name, shape, dt)
                P.dma("sp", lambda e: e.dma_start(out=d, in_=ap), reads=keys, sem="d_dbg")
            dump("nT", nT[:, :, :], [128, 16, NCOL], BF16, nTk)
            dump("R2", R2[:, :], [128, 5 * D], F32, R2A + R2H)
            dump("R3", R3[:, :, :], [128, 16, NMAIN], BF16, R3A + R3B + R3C)
            dump("R4", R4[:, :, :], [128, 16, NMAIN], BF16, R4A + R4B + R4C)
        P.emit()
    return nc


def _prep_inputs(x_prompt, x_sample, cache_k, cache_v, state_conv, norm1_g, w_in, b_gate, sink,
                 w_attn_o, conv_w, conv_b, cln_g, cln_b, w_conv_o, w_out, norm2_g, w_up, w_down, norm_f_g):
    f = lambda a: np.ascontiguousarray(np.asarray(a, dtype=np.float32))
    x_prompt, x_sample = f(x_prompt), f(x_sample)
    cache_k, cache_v, state_conv = f(cache_k)[0], f(cache_v)[0], f(state_conv)[0]
    fm = lambda v, nk: np.ascontiguousarray(f(v).reshape(nk, 128).T)
    vecs = np.zeros((128, V_N), np.float32)
    vecs[:, V_G1:V_G1 + 16] = fm(norm1_g[0], 16)
    vecs[:, V_G2:V_G2 + 16] = fm(norm2_g[0], 16)
    vecs[:, V_BG:V_BG + 32] = fm(b_gate[0], 32)
    cw = f(conv_w)[0]
    vecs[:, V_CW:V_CW + 248] = cw.reshape(31, 8, 128).transpose(2, 1, 0).reshape(128, 248)
    vecs[:, V_CB:V_CB + 8] = fm(conv_b[0], 8)
    vecs[:, V_LG:V_LG + 8] = fm(cln_g[0], 8)
    vecs[:, V_LB:V_LB + 8] = fm(cln_b[0], 8)
    vecs[:, V_SK:V_SK + 16] = np.broadcast_to(f(sink)[0][None, :], (128, 16))
    gfb = np.ascontiguousarray(np.broadcast_to(f(norm_f_g)[None, :], (128, D)))
    ident = np.eye(128, dtype=np.float32)
    j = np.arange(128)[:, None]
    i = np.arange(128)[None, :]
    m_prev = np.where(j > i, 0.0, NEG).astype(np.float32)
    m_own = np.where(j <= i, 0.0, NEG).astype(np.float32)
    t = np.arange(64)[None, :] % 4
    m_cache = np.where(np.arange(128)[:, None] >= t + 1, 0.0, NEG).astype(np.float32)
    kb, kt = np.arange(64)[:, None] // 4, np.arange(64)[:, None] % 4
    col = np.arange(256)[None, :]
    qb, qt = col // 16, col % 4
    m_new = np.full((128, 256), NEG, np.float32)
    m_new[:64] = np.where((kb == qb) & (kt <= qt), 0.0, NEG)
    m_cache = np.where(np.arange(128)[:, None] >= (col % 4) + 1, 0.0, NEG).astype(np.float32)
    common = dict(
        ident=ident, vecs=vecs, gfb=gfb,
        w_in=f(w_in)[0], w_ao=f(w_attn_o)[0], w_co=f(w_conv_o)[0], w_out=f(w_out)[0],
        w_up=f(w_up)[0], w_dn=f(w_down)[0])
    in_maps = []
    for c in range(NCORES):
        b, qd = c // 4, c % 4
        s0 = qd * 2048
        xpc = np.zeros((17 * 128, D), np.float32)
        if qd > 0:
            xpc[0:128] = x_prompt[b, s0 - 128:s0]
        xpc[128:] = x_prompt[b, s0:s0 + 2048]
        mk = np.zeros((128, 5, 512), np.float32)
        mk[:, 0, :] = np.tile(m_prev if qd > 0 else np.full((128, 128), NEG, np.float32), (1, 4))
        mk[:, 1, :] = np.tile(m_own, (1, 4))
        mk[:, 2, 0:256] = m_cache
        mk[:, 3, 0:256] = m_new
        mk[:, 4, :] = np.tile(m_prev, (1, 4))
        d = dict(common)
        d.update(
            xp=xpc, xs=np.ascontiguousarray(x_sample[16 * c:16 * c + 16].reshape(64, D)),
            ck=np.ascontiguousarray(cache_k[16 * c:16 * c + 16].reshape(16, 128, 256)),
            cv=np.ascontiguousarray(cache_v[16 * c:16 * c + 16].reshape(16, 128, 256)),
            st=np.ascontiguousarray(state_conv[16 * c:16 * c + 16]), masks=mk)
        in_maps.append(d)
    return in_maps


def kernel(**inputs):
    in_maps = _prep_inputs(**inputs)
    nc = build()
    res = run_bass_kernel_spmd(nc, in_maps, core_ids=list(range(NCORES)))
    r = res.results
    y_prompt = np.zeros((2, 8192, D), np.float32)
    for c in range(NCORES):
        y_prompt[c // 4, (c % 4) * 2048:(c % 4 + 1) * 2048] = r[c]["yp"]
    y_sample = np.concatenate([r[c]["ys"].reshape(16, 4, D) for c in range(NCORES)], axis=0)
    nkp = np.stack([r[3]["kp"], r[7]["kp"]]).reshape(1, 2, 128, 4, 64)
    nvp = np.stack([r[3]["vp"], r[7]["vp"]]).reshape(1, 2, 128, 4, 64)
    ncp = np.stack([r[3]["cp"], r[7]["cp"]]).reshape(1, 2, 30, 1024)
    nks = np.concatenate([r[c]["ksn"] for c in range(NCORES)], axis=0).reshape(1, 128, 128, 4, 64)
    nvs = np.concatenate([r[c]["vsn"] for c in range(NCORES)], axis=0).reshape(1, 128, 128, 4, 64)
    ncs = np.concatenate([r[c]["csn"] for c in range(NCORES)], axis=0).reshape(1, 128, 30, 1024)
    f = lambda a: np.ascontiguousarray(a, dtype=np.float32)
    return (f(y_prompt), f(y_sample), f(nkp), f(nvp), f(ncp), f(nks), f(nvs), f(ncs))
```
